# Optimizing a Trainium2 kernel written in Bass

```python
import math
import jax, jax.numpy as jnp
from jax import lax
import numpy as np

D_MODEL = 2048
BATCH = 2
SEQ = 16384
DEPTH = 4
DEC_BATCH = 2
DEC_SEQ = 8192
PAST_LEN = 128

HEAD_DIM = 64
A_HEADS = 12
A_KV_HEADS = 4
A_GROUP = A_HEADS // A_KV_HEADS
WINDOW = 128
BLOCK = 128
N_BUCKETS = 32
MAX_DIST = 128
B_GROUPS = 8
CHUNK = 128
C_HEADS = 12
GRID_W = 64
NA_ROWS_MAX = 8
NA_COLS = 16
NA_QC = 16
NA_KW = 32

A_WIDTH = A_HEADS * HEAD_DIM
B_WIDTH = B_GROUPS * HEAD_DIM
C_WIDTH = C_HEADS * HEAD_DIM
MIX_WIDTH = A_WIDTH + B_WIDTH + C_WIDTH
KV_WIDTH = A_KV_HEADS * HEAD_DIM
IN_SPLITS = (A_WIDTH, KV_WIDTH, KV_WIDTH, B_WIDTH, B_WIDTH, C_WIDTH, C_WIDTH, C_WIDTH)
IN_WIDTH = sum(IN_SPLITS)
D_FF = 5504
CONV_W = 3
EPS = 1e-6
NEG = -1e30

kernel_name = "hymba_style_window_gmlp_natten_encoder"


def rms_norm(x, g):
    xf = x.astype(jnp.float32)
    y = xf * lax.rsqrt(jnp.mean(xf * xf, axis=-1, keepdims=True) + EPS) * g.astype(jnp.float32)
    return y.astype(x.dtype)


def layer_norm(x, g, b):
    xf = x.astype(jnp.float32)
    mu = jnp.mean(xf, axis=-1, keepdims=True)
    var = jnp.mean(jnp.square(xf - mu), axis=-1, keepdims=True)
    y = (xf - mu) * lax.rsqrt(var + EPS) * g.astype(jnp.float32) + b.astype(jnp.float32)
    return y.astype(x.dtype)


def t5_bucket(rel):
    nb = N_BUCKETS // 2
    max_exact = nb // 2
    ret = (rel > 0).astype(jnp.int32) * nb
    n = jnp.abs(rel)
    nf = jnp.maximum(n, 1).astype(jnp.float32)
    large = max_exact + (jnp.log(nf / max_exact) / math.log(MAX_DIST / max_exact) * (nb - max_exact)).astype(jnp.int32)
    large = jnp.minimum(large, nb - 1)
    return ret + jnp.where(n < max_exact, n, large)


def window_attention(q, k, v, sink, rel_bias):
    bsz, t = q.shape[0], q.shape[1]
    nb = t // BLOCK
    s_len = 3 * BLOCK
    qb = q.reshape(bsz, nb, BLOCK, A_KV_HEADS, A_GROUP, HEAD_DIM)

    def band(z):
        zp = jnp.pad(z, ((0, 0), (BLOCK, BLOCK), (0, 0), (0, 0))).reshape(bsz, nb + 2, BLOCK, A_KV_HEADS, HEAD_DIM)
        return jnp.concatenate([zp[:, :-2], zp[:, 1:-1], zp[:, 2:]], axis=2)

    kb, vb = band(k), band(v)
    scale = HEAD_DIM ** -0.5
    s = jnp.einsum('bnqkgd,bnskd->bnkgqs', qb, kb).astype(jnp.float32) * scale
    rel = jnp.arange(s_len)[None, :] - BLOCK - jnp.arange(BLOCK)[:, None]
    bias = rel_bias.astype(jnp.float32)[t5_bucket(rel)]
    bias = bias.transpose(2, 0, 1).reshape(A_KV_HEADS, A_GROUP, BLOCK, s_len)
    kpos = jnp.arange(nb)[:, None] * BLOCK + jnp.arange(s_len)[None, :] - BLOCK
    valid = (jnp.abs(rel) <= WINDOW)[None] & ((kpos >= 0) & (kpos < t))[:, None, :]
    s = jnp.where(valid[None, :, None, None], s + bias[None, None], NEG)
    sink_l = sink.astype(jnp.float32).reshape(A_KV_HEADS, A_GROUP)[None, None, :, :, None, None]
    m = jnp.maximum(jnp.max(s, axis=-1, keepdims=True), sink_l)
    p = jnp.exp(s - m)
    p = (p / (jnp.sum(p, axis=-1, keepdims=True) + jnp.exp(sink_l - m))).astype(v.dtype)
    o = jnp.einsum('bnkgqs,bnskd->bnqkgd', p, vb)
    return o.reshape(bsz, t, A_WIDTH)


def spatial_gating(u, v, ln_g, ln_b, w_s, b_s):
    bsz, t = u.shape[0], u.shape[1]
    nc = t // CHUNK
    u = jax.nn.gelu(u)
    v = layer_norm(jax.nn.gelu(v), ln_g, ln_b)
    vc = v.reshape(bsz, nc, CHUNK, B_GROUPS, HEAD_DIM)
    s = jnp.einsum('gpq,bcqgd->bcpgd', w_s, vc) + b_s.T[None, None, :, :, None]
    return u * s.reshape(bsz, t, B_WIDTH)


def neighbourhood_attention(q, k, v, rpb):
    bsz, t = q.shape[0], q.shape[1]
    rows = t // GRID_W
    kr = min(NA_ROWS_MAX, rows)
    n_cb = GRID_W // NA_QC
    r = jnp.arange(rows)
    row_start = jnp.clip(r - kr // 2, 0, rows - kr)
    key_rows = row_start[:, None] + jnp.arange(kr)[None, :]
    qcol = jnp.arange(GRID_W).reshape(n_cb, NA_QC)
    col_start = jnp.clip(qcol - NA_COLS // 2, 0, GRID_W - NA_COLS)
    win_start = jnp.clip(qcol[:, 0] - NA_COLS // 2, 0, GRID_W - NA_KW)
    key_cols = win_start[:, None] + jnp.arange(NA_KW)[None, :]
    flat = key_rows[:, None, :, None] * GRID_W + key_cols[None, :, None, :]
    kg = k[:, flat]
    vg = v[:, flat]
    qg = q.reshape(bsz, rows, n_cb, NA_QC, C_HEADS, HEAD_DIM)
    scale = HEAD_DIM ** -0.5
    s = jnp.einsum('brcqhd,brcawhd->brchqaw', qg, kg).astype(jnp.float32) * scale
    dr = key_rows - r[:, None]
    dc = key_cols[:, None, :] - qcol[:, :, None]
    kc = key_cols[:, None, :]
    colmask = (kc >= col_start[:, :, None]) & (kc < col_start[:, :, None] + NA_COLS)
    ri = dr + NA_ROWS_MAX - 1
    ci = jnp.clip(dc + NA_COLS - 1, 0, 2 * NA_COLS - 2)
    bias = rpb.astype(jnp.float32)[:, ri[:, None, None, :, None], ci[None, :, :, None, :]]
    bias = bias.transpose(1, 2, 0, 3, 4, 5)
    s = jnp.where(colmask[None, None, :, None, :, None, :], s + bias[None], NEG)
    p = jax.nn.softmax(s, axis=(-2, -1)).astype(v.dtype)
    o = jnp.einsum('brchqaw,brcawhd->brcqhd', p, vg)
    return o.reshape(bsz, t, C_WIDTH)


def conv_ffn(x, w_up, conv_w, conv_b, w_down):
    h = x @ w_up
    hp = jnp.pad(h, ((0, 0), (1, 1), (0, 0)))
    h = hp[:, :-2] * conv_w[0] + hp[:, 1:-1] * conv_w[1] + hp[:, 2:] * conv_w[2] + conv_b
    g, val = jnp.split(h, 2, axis=-1)
    return (jax.nn.gelu(g) * val) @ w_down


def run_trunk(x, rel_bias, norm1_g, w_in, sink, sgu_ln_g, sgu_ln_b, w_spatial, b_spatial,
              na_rpb, gn_a, gn_b, gn_c, w_out, norm2_g, w_up, conv_w, conv_b, w_down, final_g):
    bsz, t = x.shape[0], x.shape[1]
    offs = [int(o) for o in np.cumsum(IN_SPLITS)[:-1]]
    h = x
    for l in range(DEPTH):
        xn = rms_norm(h, norm1_g[l])
        proj = xn @ w_in[l]
        qa, ka, va, ub, vb, qc, kc, vc = jnp.split(proj, offs, axis=-1)
        ya = window_attention(qa.reshape(bsz, t, A_HEADS, HEAD_DIM),
                              ka.reshape(bsz, t, A_KV_HEADS, HEAD_DIM),
                              va.reshape(bsz, t, A_KV_HEADS, HEAD_DIM), sink[l], rel_bias)
        yb = spatial_gating(ub, vb, sgu_ln_g[l], sgu_ln_b[l], w_spatial[l], b_spatial[l])
        yc = neighbourhood_attention(qc.reshape(bsz, t, C_HEADS, HEAD_DIM),
                                     kc.reshape(bsz, t, C_HEADS, HEAD_DIM),
                                     vc.reshape(bsz, t, C_HEADS, HEAD_DIM), na_rpb[l])
        merged = jnp.concatenate([rms_norm(ya, gn_a[l]), rms_norm(yb, gn_b[l]), rms_norm(yc, gn_c[l])], axis=-1)
        h = h + merged @ w_out[l]
        h = h + conv_ffn(rms_norm(h, norm2_g[l]), w_up[l], conv_w[l], conv_b[l], w_down[l])
    return rms_norm(h, final_g)


def setup_inputs(seed: int = 0) -> dict:
    key = jax.random.key(seed)
    ks = jax.random.split(key, 24)
    f32 = jnp.float32
    nrm = lambda k, shape: jax.random.normal(k, shape, f32)
    res_scale = (2.0 * DEPTH) ** -0.5
    return {
        "x_prompt": nrm(ks[0], (BATCH, SEQ, D_MODEL)),
        "x_sample": nrm(ks[1], (DEC_BATCH, DEC_SEQ, D_MODEL)),
        "rel_bias": 0.2 * nrm(ks[2], (N_BUCKETS, A_HEADS)),
        "norm1_g": 1.0 + 0.02 * nrm(ks[3], (DEPTH, D_MODEL)),
        "w_in": nrm(ks[4], (DEPTH, D_MODEL, IN_WIDTH)) * D_MODEL ** -0.5,
        "sink": 0.5 * nrm(ks[5], (DEPTH, A_HEADS)),
        "sgu_ln_g": 1.0 + 0.02 * nrm(ks[6], (DEPTH, B_WIDTH)),
        "sgu_ln_b": 0.02 * nrm(ks[7], (DEPTH, B_WIDTH)),
        "w_spatial": nrm(ks[8], (DEPTH, B_GROUPS, CHUNK, CHUNK)) * CHUNK ** -0.5,
        "b_spatial": 0.02 * nrm(ks[9], (DEPTH, B_GROUPS, CHUNK)),
        "na_rpb": 0.2 * nrm(ks[10], (DEPTH, C_HEADS, 2 * NA_ROWS_MAX - 1, 2 * NA_COLS - 1)),
        "gn_a": 1.0 + 0.02 * nrm(ks[11], (DEPTH, A_WIDTH)),
        "gn_b": 1.0 + 0.02 * nrm(ks[12], (DEPTH, B_WIDTH)),
        "gn_c": 1.0 + 0.02 * nrm(ks[13], (DEPTH, C_WIDTH)),
        "w_out": nrm(ks[14], (DEPTH, MIX_WIDTH, D_MODEL)) * (MIX_WIDTH ** -0.5) * res_scale,
        "norm2_g": 1.0 + 0.02 * nrm(ks[15], (DEPTH, D_MODEL)),
        "w_up": nrm(ks[16], (DEPTH, D_MODEL, 2 * D_FF)) * D_MODEL ** -0.5,
        "conv_w": nrm(ks[17], (DEPTH, CONV_W, 2 * D_FF)) * CONV_W ** -0.5,
        "conv_b": 0.02 * nrm(ks[18], (DEPTH, 2 * D_FF)),
        "w_down": nrm(ks[19], (DEPTH, D_FF, D_MODEL)) * (D_FF ** -0.5) * res_scale,
        "final_g": 1.0 + 0.02 * nrm(ks[20], (D_MODEL,)),
    }


def reference(x_prompt, x_sample, rel_bias, norm1_g, w_in, sink, sgu_ln_g, sgu_ln_b, w_spatial, b_spatial,
              na_rpb, gn_a, gn_b, gn_c, w_out, norm2_g, w_up, conv_w, conv_b, w_down, final_g):
    y_prompt = run_trunk(x_prompt, rel_bias, norm1_g, w_in, sink, sgu_ln_g, sgu_ln_b, w_spatial, b_spatial,
                         na_rpb, gn_a, gn_b, gn_c, w_out, norm2_g, w_up, conv_w, conv_b, w_down, final_g)
    y_sample = run_trunk(x_sample, rel_bias, norm1_g, w_in, sink, sgu_ln_g, sgu_ln_b, w_spatial, b_spatial,
                         na_rpb, gn_a, gn_b, gn_c, w_out, norm2_g, w_up, conv_w, conv_b, w_down, final_g)
    return (y_prompt, y_sample)
```

```python
import math
from contextlib import ExitStack
import numpy as np
import ml_dtypes
import concourse.bass as bass
import concourse.mybir as mybir
from concourse.bass_utils import run_bass_kernel_spmd

F32 = mybir.dt.float32
BF16 = mybir.dt.bfloat16
AF = mybir.ActivationFunctionType
ALU = mybir.AluOpType

D = 2048
KC = 16
DFF = 5504
NFC = 86
HD = 64
EPS = 1e-6
HALO_L = 384
QA_ORDER = [0, 3, 1, 4, 2, 5, 6, 9, 7, 10, 8, 11]
NPIECE = 47
P_IN, P_OUT, P_UP, P_DN = 0, 9, 13, 35


class Tok:
    __slots__ = ("sem", "val", "eng", "tiny")

    def __init__(self, sem, val, eng, tiny=False):
        self.sem, self.val, self.eng, self.tiny = sem, val, eng, tiny


class DSem:
    def __init__(self, h):
        self.h, self.n = h, 0


class Buf:
    def __init__(self, name, t=None):
        self.name, self.t = name, t
        self.w = None
        self.r = []
        self.dsem = None

    def __getitem__(self, k):
        return self.t[k]


class Eng:
    def __init__(self, name, e, sem, compute=True):
        self.name, self.e, self.sem, self.compute = name, e, sem, compute
        self.n = 0
        self.seen = {}

    def wait(self, tok):
        if tok is None:
            return
        if self.compute and tok.eng is self and not tok.tiny:
            return
        k = id(tok.sem)
        if self.seen.get(k, 0) >= tok.val:
            return
        self.e.wait_ge(tok.sem, tok.val)
        self.seen[k] = tok.val


class S:
    def __init__(self, nc):
        self.nc = nc
        self.pe = Eng("pe", nc.tensor, nc.alloc_semaphore("s_pe"))
        self.act = Eng("act", nc.scalar, nc.alloc_semaphore("s_act"))
        self.dve = Eng("dve", nc.vector, nc.alloc_semaphore("s_dve"))
        self.pool = Eng("pool", nc.gpsimd, nc.alloc_semaphore("s_pool"))
        self.sp = Eng("sp", nc.sync, None, compute=False)
        self.engs = [self.pe, self.act, self.dve, self.pool, self.sp]
        self.free_ds = []
        self.all_ds = []
        self.dram = {}

    def dbuf(self, key):
        b = self.dram.get(key)
        if b is None:
            b = Buf(str(key))
            self.dram[key] = b
        return b

    def _deps(self, reads, writes):
        toks = []
        for b in reads:
            if b.w is not None:
                toks.append(b.w)
        for b in writes:
            if b.w is not None:
                toks.append(b.w)
            toks.extend(b.r)
        return toks

    def _commit(self, tok, reads, writes):
        for b in reads:
            b.r.append(tok)
            if len(b.r) > 24:
                b.r = b.r[-24:]
        for b in writes:
            b.w = tok
            b.r = []

    def op(self, eng, fns, reads=(), writes=(), tiny=False):
        if callable(fns):
            fns = [fns]
        for t in self._deps(reads, writes):
            eng.wait(t)
        ins = None
        for f in fns:
            ins = f(eng.e)
        eng.n += 1
        ins.then_inc(eng.sem, 1)
        tok = Tok(eng.sem, eng.n, eng, tiny)
        self._commit(tok, reads, writes)
        return tok

    def _getds(self, b):
        if b.dsem is None:
            if self.free_ds:
                b.dsem = self.free_ds.pop()
            else:
                b.dsem = DSem(self.nc.alloc_semaphore("d%d" % len(self.all_ds)))
                self.all_ds.append(b.dsem)
        return b.dsem

    def dma(self, q, out, in_, sb, reads=(), writes=()):
        for t in self._deps(reads, writes):
            q.wait(t)
        ds = self._getds(sb)
        ds.n += 1
        q.e.dma_start(out=out, in_=in_).then_inc(ds.h, 16)
        tok = Tok(ds.h, 16 * ds.n, None)
        self._commit(tok, reads, writes)
        return tok

    def release(self, bufs):
        for b in bufs:
            if b.dsem is not None:
                self.free_ds.append(b.dsem)
                b.dsem = None

    def barrier(self):
        toks = [Tok(d.h, 16 * d.n, None) for d in self.all_ds if d.n > 0]
        toks += [Tok(e.sem, e.n, e) for e in self.engs if e.compute and e.n > 0]
        for e in self.engs:
            for t in toks:
                e.wait(t)


def _t5_bucket(rel):
    nb = 16
    max_exact = 8
    ret = (rel > 0).astype(np.int64) * nb
    n = np.abs(rel)
    nf = np.maximum(n, 1).astype(np.float32)
    large = max_exact + (np.log(nf / max_exact) / math.log(128 / max_exact) * (nb - max_exact)).astype(np.int64)
    large = np.minimum(large, nb - 1)
    return ret + np.where(n < max_exact, n, large)


def _bf16(a):
    return np.ascontiguousarray(a).astype(ml_dtypes.bfloat16)


def _host_weights(inp, depth):
    cols_feat = []
    for i in range(6):
        a, b = QA_ORDER[2 * i], QA_ORDER[2 * i + 1]
        cols_feat += list(range(a * 64, a * 64 + 64)) + list(range(b * 64, b * 64 + 64))
    cols_feat += list(range(768, 1024))
    cols_feat += list(range(2304, 3072))
    cols_feat += list(range(3072, 3840))
    cols_tok = list(range(1280, 1792)) + list(range(1792, 2304)) + list(range(3840, 4608)) + list(range(1024, 1280))
    cols_in = np.array(cols_feat + cols_tok)
    up_cols = []
    for i in range(43):
        up_cols += list(range(i * 128, i * 128 + 128)) + list(range(DFF + i * 128, DFF + i * 128 + 128))
    up_cols = np.array(up_cols)
    W = np.zeros((depth, NPIECE, 128, 8192), np.float32)
    for l in range(depth):
        wi = inp["w_in"][l][:, cols_in].reshape(16, 128, 9, 512).transpose(2, 1, 0, 3)
        W[l, P_IN:P_IN + 9] = wi.reshape(9, 128, 8192)
        wo = inp["w_out"][l].reshape(16, 128, 4, 512).transpose(2, 1, 0, 3)
        W[l, P_OUT:P_OUT + 4] = wo.reshape(4, 128, 8192)
        wu = np.zeros((D, 22 * 512), np.float32)
        wu[:, :2 * DFF] = inp["w_up"][l][:, up_cols]
        W[l, P_UP:P_UP + 22] = wu.reshape(16, 128, 22, 512).transpose(2, 1, 0, 3).reshape(22, 128, 8192)
        wd = np.zeros((48 * 128, D), np.float32)
        wd[:DFF] = inp["w_down"][l]
        wd = wd.reshape(3, 16, 128, 4, 512).transpose(3, 0, 2, 1, 4)
        W[l, P_DN:P_DN + 12] = wd.reshape(12, 128, 8192)
    gcol = np.zeros((depth, 128, 48), np.float32)
    cw = np.zeros((depth, 128, NFC * 4), np.float32)
    for l in range(depth):
        gm = np.concatenate([inp["gn_a"][l], inp["gn_b"][l], inp["gn_c"][l]])
        for j, g in enumerate([inp["norm1_g"][l], gm, inp["norm2_g"][l]]):
            gcol[l, :, 16 * j:16 * j + 16] = g.reshape(16, 128).T
        c4 = np.concatenate([inp["conv_w"][l], inp["conv_b"][l][None]], 0)[:, up_cols]
        cw[l] = c4.reshape(4, NFC, 128).transpose(2, 1, 0).reshape(128, NFC * 4)
    return W, gcol, cw


def _host_tables(inp, depth):
    s = np.arange(384)[:, None]
    q = np.arange(128)[None, :]
    rel = s - 128 - q
    tab = inp["rel_bias"][_t5_bucket(rel)]
    tab = tab[:, :, QA_ORDER]
    tabA = tab.reshape(3, 128, 128, 12).transpose(1, 3, 0, 2)
    tabA = np.ascontiguousarray(tabA).reshape(128, 12 * 384).astype(np.float32)
    mA = (np.abs(rel) <= 128).astype(np.float32).reshape(3, 128, 128).transpose(1, 0, 2).reshape(128, 384)
    mA = np.ascontiguousarray(np.broadcast_to(mA[:, None, :], (128, 12, 384))).reshape(128, 12 * 384)
    sr = (np.arange(128) // 64)[:, None, None]
    sc = (np.arange(128) % 64)[:, None, None]
    cp = (np.arange(7) - 3)[None, :, None]
    qr = (np.arange(128) // 64)[None, None, :]
    qc = (np.arange(128) % 64)[None, None, :]
    dr = 2 * cp + sr - qr
    ri = np.clip(dr + 7, 0, 14)
    ci = np.clip(sc - qc + 15, 0, 30)
    ri, ci = np.broadcast_arrays(ri, ci)
    tabC = np.zeros((depth, 128, 12 * 7 * 128), np.float32)
    for l in range(depth):
        t = inp["na_rpb"][l][:, ri, ci]
        tabC[l] = t.transpose(1, 0, 2, 3).reshape(128, -1)
    cstart = np.clip(np.arange(64) - 8, 0, 48)
    scv = (np.arange(128) % 64)[:, None]
    qcv = np.arange(128) % 64
    cm = ((scv >= cstart[qcv][None, :]) & (scv < cstart[qcv][None, :] + 16)).astype(np.float32)
    mC = np.ascontiguousarray(np.broadcast_to(cm[:, None, :], (128, 84, 128))).reshape(128, 84 * 128)
    srv = (np.arange(128) // 64)[:, None]
    qrv = (np.arange(128) // 64)[None, :]
    bad = (srv == 0) & (qrv == 1)
    e_lo = cm * (~bad).astype(np.float32)
    e_hi = cm * bad.astype(np.float32)
    mE = np.stack([e_lo, e_hi], 1)
    mE = np.ascontiguousarray(np.broadcast_to(mE[:, None], (128, 12, 2, 128))).reshape(128, 24 * 128).astype(np.float32)
    return tabA, mA, tabC, mC, mE


def _na_rowmask(seq_rows, row0, ntiles):
    m = np.zeros((128, ntiles, 7, 2), np.float32)
    for j in range(ntiles):
        for qr in range(2):
            r = row0 + 2 * j + qr
            if 0 <= r < seq_rows:
                st = min(max(r - 4, 0), seq_rows - 8)
            else:
                st = r - 4
            for c in range(7):
                for sr in range(2):
                    kr = row0 + 2 * (j + c - 3) + sr
                    if st <= kr < st + 8:
                        m[sr * 64:(sr + 1) * 64, j, c, qr] = 1.0
    return m.reshape(128, ntiles * 14)


_STOP = [None]
_DBG = set()


def build(depth, Ts):
    nc = bass.Bass("TRN2", target_bir_lowering=False)
    H = HALO_L * depth
    nseg = len(Ts)
    Es = [T + 2 * H for T in Ts]
    NTs = [E // 128 for E in Es]
    NTtot = sum(NTs)

    def din(name, shape, dt=F32):
        return nc.dram_tensor(name, list(shape), dt, kind="ExternalInput").ap()

    def dscr(name, shape, dt):
        kind = "ExternalOutput" if name in _DBG else "Internal"
        return nc.dram_tensor(name, list(shape), dt, kind=kind).ap()

    xin = [din("x%d" % s, [Es[s], D]) for s in range(nseg)]
    yout = [nc.dram_tensor("y%d" % s, [Ts[s], D], F32, kind="ExternalOutput").ap() for s in range(nseg)]
    wf32 = din("wf32", [depth, NPIECE, 128, 8192])
    gcol_d = din("gcol", [depth, 128, 48])
    cw_d = din("cw", [depth, 128, NFC * 4])
    tabA_d = din("tabA", [128, 12 * 384])
    mA_d = din("mA", [128, 12 * 384])
    tabC_d = din("tabC", [depth, 128, 84 * 128])
    mC_d = din("mC", [128, 84 * 128])
    mE_d = din("mE", [128, 24 * 128])
    rowm_d = din("rowm", [128, NTtot * 14])
    valid_d = din("valid", [128, NTtot])
    flags_d = din("flags", [128, 2])
    ident_d = din("ident", [128, 128], BF16)
    wsT_d = din("wsT", [depth, 128, 8 * 128])
    bs_d = din("bs", [depth, 128, 8])
    lng_d = din("lng", [depth, 128, 512])
    lnb_d = din("lnb", [depth, 128, 512])
    sink_d = din("sink", [depth, 128, 12])
    fg_d = din("fg", [128, D])

    wbf = [dscr("wbf%d" % l_, [NPIECE, 128, 8192], BF16) for l_ in range(depth)]
    hbuf = [dscr("hbuf%d" % s, [Es[s], D], F32) for s in range(nseg)]
    hmid = [dscr("hmid%d" % s, [Es[s], D], F32) for s in range(nseg)]
    featT = [dscr("featT%d" % s, [20, 128, Es[s]], BF16) for s in range(nseg)]
    UB = [dscr("ub%d" % s, [Es[s], 512], F32) for s in range(nseg)]
    VB = [dscr("vb%d" % s, [Es[s], 512], BF16) for s in range(nseg)]
    VCd = [dscr("vc%d" % s, [Es[s], 780], BF16) for s in range(nseg)]
    VAd = [dscr("va%d" % s, [Es[s], 260], BF16) for s in range(nseg)]

    dbgst = dscr("dbgst", [64, 128, 8], F32) if "dbgst" in _DBG else None
    dbgn = [0]

    sc = S(nc)
    pe, act, dve, pool, sp = sc.pe, sc.act, sc.dve, sc.pool, sc.sp

    def dump_stat(st):
        if dbgst is not None and dbgn[0] < 64:
            sc.dma(sp, dbgst[dbgn[0]], st[:, :], st, reads=[st], writes=[sc.dbuf(("dbg", dbgn[0]))])
            dbgn[0] += 1

    psum = nc.alloc_psum_tensor("psum", [128, 4096], F32)
    PS = [Buf("ps%d" % b) for b in range(8)]

    def psf(b, n=512, off=0):
        return psum[:, b * 512 + off: b * 512 + off + n]

    def psb(b0, ncol, off=0):
        v = psum[:, b0 * 512:(b0 + (ncol + 1023) // 1024) * 512].bitcast(BF16)
        return v[:, off:off + ncol]

    uniq = [0]

    def sb(es, name, shape, dt):
        uniq[0] += 1
        t = es.enter_context(nc.sbuf_tensor("sb%d_%s" % (uniq[0], name), list(shape), dt))
        return Buf(name, t)

    g_es = ExitStack()
    ident = sb(g_es, "ident", [128, 128], BF16)
    validt = sb(g_es, "validt", [128, NTtot], F32)
    flags = sb(g_es, "flags", [128, 2], F32)
    rowm = sb(g_es, "rowm", [128, NTtot * 14], F32)
    stat = [sb(g_es, "stat%d" % i, [128, 8], F32) for i in range(4)]
    epsb = sb(g_es, "epsb", [128, 1], F32)
    for b_, d_ in ((ident, ident_d), (validt, valid_d), (flags, flags_d), (rowm, rowm_d)):
        sc.dma(sp, b_[:], d_[:, :], b_, writes=[b_])
    sc.op(dve, lambda e: e.memset(epsb[:], EPS), writes=[epsb])

    statn = [0]

    def rstd_from_ss(ss_ap, ss_buf, n):
        st = stat[statn[0] % 4]
        statn[0] += 1
        sc.op(act, lambda e: e.activation(out=st[:, 1:2], in_=ss_ap, func=AF.Sqrt, bias=epsb[:, 0:1], scale=1.0 / n),
              reads=[ss_buf, epsb], writes=[st])
        sc.op(dve, lambda e: e.reciprocal(out=st[:, 2:3], in_=st[:, 1:2]), reads=[st], writes=[st], tiny=True)
        return st, st[:, 2:3]

    with ExitStack() as es:
        cin = [sb(es, "cin%d" % i, [128, 8192], F32) for i in range(2)]
        cout = [sb(es, "cout%d" % i, [128, 8192], BF16) for i in range(2)]
        gc = sb(es, "gc", [128, 48 * depth], F32)
        for l in range(depth):
            sc.dma(sp, gc[:, 48 * l:48 * l + 48], gcol_d[l], gc, writes=[gc])
        plist = [(0, p) for p in range(min(NPIECE, _STOP[1] if len(_STOP) > 1 else NPIECE))]

        def cast_load(n):
            l_, p_ = plist[n]
            ci_ = cin[n % 2]
            sc.dma((sp, act)[n % 2], ci_[:], wf32[l_, p_, :, :], ci_, writes=[ci_])

        cast_load(0)
        for n, (l, p) in enumerate(plist):
            if n + 1 < len(plist):
                cast_load(n + 1)
            ci_, co_ = cin[n % 2], cout[n % 2]
            goff = None if p >= P_DN else (0 if p < P_OUT else (16 if p < P_UP else 32))
            for (eng, k0, k1) in ((dve, 0, 10), (pool, 10, 16)):
                if goff is None:
                    sc.op(eng, lambda e, ci_=ci_, co_=co_, k0=k0, k1=k1: e.tensor_copy(out=co_[:, k0 * 512:k1 * 512], in_=ci_[:, k0 * 512:k1 * 512]),
                          reads=[ci_], writes=[co_])
                else:
                    fns = []
                    for k in range(k0, k1):
                        gap = gc[:, 48 * l + goff + k: 48 * l + goff + k + 1]
                        fns.append(lambda e, k=k, gap=gap, ci_=ci_, co_=co_: e.tensor_scalar(
                            out=co_[:, k * 512:(k + 1) * 512], in0=ci_[:, k * 512:(k + 1) * 512],
                            scalar1=gap, scalar2=None, op0=ALU.mult))
                    sc.op(eng, fns, reads=[ci_, gc], writes=[co_])
            sc.dma(sp, wbf[l][p, :, :], co_[:], co_, reads=[co_], writes=[sc.dbuf(("w", l, p))])
        sc.barrier()
        sc.release(cin + cout + [gc])

    wtoks = {}
    if _STOP[0] == 'cast':
        sc.barrier()
        return nc

    def wpiece_load(wb, l, p, ncols=8192):
        sc.dma(sp, wb[:, 0:ncols], wbf[l][p, :, 0:ncols], wb, reads=[sc.dbuf(("w", l, p))], writes=[wb])

    def norm_tile(src_ap, src_deps, hb, xs, valid_ap=None):
        sc.dma(sp, hb[:], src_ap, hb, reads=src_deps, writes=[hb])
        st = stat[statn[0] % 4]
        sc.op(act, lambda e: e.activation(out=xs[:], in_=hb[:], func=AF.Square, accum_out=st[:, 0:1]),
              reads=[hb], writes=[xs, st])
        st2, r = rstd_from_ss(st[:, 0:1], st, D)
        sc.op(dve, lambda e: e.tensor_scalar(out=xs[:], in0=hb[:], scalar1=r, scalar2=None, op0=ALU.mult),
              reads=[hb, st2], writes=[xs])
        dump_stat(st2)

    def transpose_tile(xs, xT, col0, banks, ev=0):
        b0 = banks[0]
        pv = psb(b0, 2048)
        sc.op(pe, [lambda e, k=k: e.transpose(out=pv[:, k * 128:(k + 1) * 128], in_=xs[:, k * 128:(k + 1) * 128],
                                              identity=ident[:]) for k in range(16)],
              reads=[xs, ident], writes=[PS[b0], PS[b0 + 1]])
        xv = xT[:].rearrange("p (k t) -> p k t", k=16)
        for hlf in range(2):
            eng = (act, dve)[(hlf + ev) % 2]
            src = pv[:, hlf * 1024:(hlf + 1) * 1024].rearrange("p (k t) -> p k t", k=8)
            dst = xv[:, hlf * 8:(hlf + 1) * 8, col0:col0 + 128]
            if eng is act:
                sc.op(act, lambda e, src=src, dst=dst: e.activation(out=dst, in_=src, func=AF.Copy),
                      reads=[PS[b0 + hlf]], writes=[xT])
            else:
                sc.op(dve, lambda e, src=src, dst=dst: e.tensor_copy(out=dst, in_=src),
                      reads=[PS[b0 + hlf]], writes=[xT])

    def htok(seg, tile_):
        return sc.dbuf(("h", seg, tile_))

    def rows_deps(kind, seg, r0, r1):
        return [sc.dbuf((kind, seg, t)) for t in range(r0 // 128, (r1 - 1) // 128 + 1)]

    tile_base = [0]
    for s_ in range(nseg - 1):
        tile_base.append(tile_base[-1] + NTs[s_])

    class Caster:
        def __init__(self):
            self.jobs, self.nxt, self.loaded, self.bufs = [], 0, None, None

        def start_layer(self, ln):
            self.jobs = [(ln, p, qt) for p in range(NPIECE) for qt in range(4)]
            self.nxt, self.loaded = 0, None

        def _load(self, n):
            ln, p, qt = self.jobs[n]
            ci_ = self.bufs[0][n % 2]
            sc.dma(sp, ci_[:], wf32[ln, p, :, qt * 2048:(qt + 1) * 2048], ci_, writes=[ci_])

        def _cast(self, n):
            ln, p, qt = self.jobs[n]
            ci_, co_, gc_ = self.bufs[0][n % 2], self.bufs[1][n % 2], self.bufs[2]
            goff = None if p >= P_DN else (0 if p < P_OUT else (16 if p < P_UP else 32))
            for (eng, i0, i1) in ((dve, 0, 2), (pool, 2, 4)):
                if goff is None:
                    sc.op(eng, lambda e, i0=i0, i1=i1: e.tensor_copy(out=co_[:, i0 * 512:i1 * 512], in_=ci_[:, i0 * 512:i1 * 512]),
                          reads=[ci_], writes=[co_])
                else:
                    fns = []
                    for i in range(i0, i1):
                        k = qt * 4 + i
                        gap = gc_[:, 48 * ln + goff + k: 48 * ln + goff + k + 1]
                        fns.append(lambda e, i=i, gap=gap: e.tensor_scalar(out=co_[:, i * 512:(i + 1) * 512], in0=ci_[:, i * 512:(i + 1) * 512],
                                                                          scalar1=gap, scalar2=None, op0=ALU.mult))
                    sc.op(eng, fns, reads=[ci_, gc_], writes=[co_])
            sc.dma(pool, wbf[ln][p, :, qt * 2048:(qt + 1) * 2048], co_[:], co_, reads=[co_], writes=[sc.dbuf(("w", ln, p))])

        def point(self):
            if self.bufs is None:
                return
            tc = self.loaded
            if self.nxt < len(self.jobs):
                self._load(self.nxt)
                self.loaded = self.nxt
                self.nxt += 1
            else:
                self.loaded = None
            if tc is not None:
                self._cast(tc)

        def drain(self):
            if self.bufs is not None and self.loaded is not None:
                self._cast(self.loaded)
                self.loaded = None

        def flush(self):
            while self.bufs is not None and (self.nxt < len(self.jobs) or self.loaded is not None):
                self.point()

    caster = Caster()

    for l in range(depth):
        rem = depth - 1 - l
        a = HALO_L * rem
        last = (l == depth - 1)
        for seg in range(nseg):
            T = Ts[seg]
            E = Es[seg]
            f0, f1 = H - a, H + T + a
            m0, m1 = f0 - 128, f1 + 128
            p0, p1 = m0 - 256, m1 + 256
            hsrc = xin[seg] if l == 0 else hbuf[seg]
            hkind = "x" if l == 0 else "h"

            with ExitStack() as es:
                hb = [sb(es, "hb%d" % i, [128, D], F32) for i in range(2)]
                xs = [sb(es, "xs%d" % i, [128, D], BF16) for i in range(2)]
                xT = [sb(es, "xT%d" % i, [128, 16 * 512], BF16) for i in range(2)]
                wb = [sb(es, "wb%d" % i, [128, 8192], BF16) for i in range(2)]
                stf = [sb(es, "stf%d" % i, [128, 512], BF16) for i in range(3)]
                stt = [sb(es, "stt%d" % i, [128, 512], F32) for i in range(2)]
                stb = [sb(es, "stb%d" % i, [128, 512], BF16) for i in range(2)]
                g1 = sb(es, "g1", [128, 512], F32)
                g2 = sb(es, "g2", [128, 512], F32)
                vcs = [sb(es, "vcs%d" % i, [128, 780], BF16) for i in range(4)]
                vas = [sb(es, "vas%d" % i, [128, 260], BF16) for i in range(4)]
                lng = sb(es, "lng", [128, 512], F32)
                lnb = sb(es, "lnb", [128, 512], F32)
                allb = hb + xs + xT + wb + stf + stt + stb + [g1, g2, lng, lnb] + vcs + vas
                sc.dma(sp, lng[:], lng_d[l], lng, writes=[lng])
                sc.dma(sp, lnb[:], lnb_d[l], lnb, writes=[lnb])
                ntile = (p1 - p0) // 128
                gi = 0
                nw = 0
                nst = 0
                bank = 0
                for gt in range(0, ntile, 4):
                    ng = min(4, ntile - gt)
                    N = 128 * ng
                    t0 = p0 + gt * 128
                    xt_ = xT[gi % 2]
                    xv = xt_[:].rearrange("p (k t) -> p k t", k=16)
                    for i in range(ng):
                        r0 = t0 + i * 128
                        norm_tile(hsrc[r0:r0 + 128, :], rows_deps(hkind, seg, r0, r0 + 128), hb[i % 2], xs[i % 2])
                        transpose_tile(xs[i % 2], xt_, i * 128, (0, 1) if i % 2 == 0 else (2, 3), ev=i)
                    for p in range(5):
                        w_ = wb[nw % 2]
                        nw += 1
                        wpiece_load(w_, l, P_IN + p)
                        wv = w_[:].rearrange("p (k c) -> p k c", k=16)
                        for cc in range(4):
                            c = 4 * p + cc
                            bk = 4 + bank % 4
                            bank += 1
                            sc.op(pe, [lambda e, k=k, cc=cc, bk=bk, wv=wv: e.matmul(
                                psf(bk, N), wv[:, k, cc * 128:(cc + 1) * 128], xv[:, k, 0:N],
                                start=(k == 0), stop=(k == 15)) for k in range(16)],
                                reads=[w_, xt_], writes=[PS[bk]])
                            so = stf[nst % 3]
                            nst += 1
                            if c % 2 == 0:
                                sc.op(act, lambda e, so=so, bk=bk: e.activation(out=so[:, 0:N], in_=psf(bk, N), func=AF.Copy),
                                      reads=[PS[bk]], writes=[so])
                            else:
                                sc.op(dve, lambda e, so=so, bk=bk: e.tensor_copy(out=so[:, 0:N], in_=psf(bk, N)),
                                      reads=[PS[bk]], writes=[so])
                            sc.dma(pool, featT[seg][c, :, t0:t0 + N], so[:, 0:N], so, reads=[so],
                                   writes=[sc.dbuf(("f", seg, (t0 // 128) + i_)) for i_ in range(ng)])
                    for p in range(4):
                        w_ = wb[nw % 2]
                        nw += 1
                        wpiece_load(w_, l, P_IN + 5 + p)
                        wv = w_[:].rearrange("p (k c) -> p k c", k=16)
                        for i in range(ng):
                            r0 = t0 + i * 128
                            tl = r0 // 128
                            vcol = validt[:, tile_base[seg] + tl: tile_base[seg] + tl + 1]
                            bk = 4 + bank % 4
                            bank += 1
                            sc.op(pe, [lambda e, k=k, i=i, bk=bk, wv=wv: e.matmul(
                                psf(bk), xv[:, k, i * 128:(i + 1) * 128], wv[:, k, :],
                                start=(k == 0), stop=(k == 15)) for k in range(16)],
                                reads=[w_, xt_], writes=[PS[bk]])
                            if p == 0:
                                so = stt[i % 2]
                                sc.op(act, lambda e, so=so, bk=bk: e.activation(out=so[:], in_=psf(bk), func=AF.Gelu_apprx_tanh),
                                      reads=[PS[bk]], writes=[so])
                                sc.dma(pool, UB[seg][r0:r0 + 128, :], so[:], so, reads=[so], writes=[sc.dbuf(("ub", seg, tl))])
                            elif p == 1:
                                st = stat[statn[0] % 4]
                                statn[0] += 1
                                sc.op(act, lambda e, bk=bk, st=st: e.activation(out=g1[:], in_=psf(bk), func=AF.Gelu_apprx_tanh,
                                                                             accum_out=st[:, 0:1]),
                                      reads=[PS[bk]], writes=[g1, st])
                                sc.op(act, lambda e, st=st: e.activation(out=g2[:], in_=g1[:], func=AF.Square, accum_out=st[:, 1:2]),
                                      reads=[g1], writes=[g2, st])
                                sc.op(act, lambda e, st=st: e.activation(out=st[:, 7:8], in_=st[:, 1:2], func=AF.Copy),
                                      reads=[st], writes=[st])
                                sc.op(dve, [
                                    lambda e, st=st: e.tensor_scalar(out=st[:, 2:3], in0=st[:, 0:1], scalar1=1.0 / 512, scalar2=None, op0=ALU.mult),
                                    lambda e, st=st: e.tensor_tensor(out=st[:, 3:4], in0=st[:, 0:1], in1=st[:, 0:1], op=ALU.mult),
                                ], reads=[st], writes=[st], tiny=True)
                                sc.op(dve, lambda e, st=st: e.scalar_tensor_tensor(out=st[:, 4:5], in0=st[:, 3:4], scalar=-1.0 / 512, in1=st[:, 1:2],
                                                                                  op0=ALU.mult, op1=ALU.add),
                                      reads=[st], writes=[st], tiny=True)
                                sc.op(act, lambda e, st=st: e.activation(out=st[:, 5:6], in_=st[:, 4:5], func=AF.Sqrt, bias=epsb[:, 0:1], scale=1.0 / 512),
                                      reads=[st, epsb], writes=[st])
                                so = stb[i % 2]
                                sc.op(dve, lambda e, st=st: e.reciprocal(out=st[:, 6:7], in_=st[:, 5:6]), reads=[st], writes=[st], tiny=True)
                                sc.op(dve, [
                                    lambda e, st=st: e.tensor_scalar(out=g2[:], in0=g1[:], scalar1=st[:, 2:3], scalar2=st[:, 6:7],
                                                                     op0=ALU.subtract, op1=ALU.mult),
                                    lambda e: e.tensor_tensor(out=g2[:], in0=g2[:], in1=lng[:], op=ALU.mult),
                                    lambda e, so=so: e.tensor_tensor(out=so[:], in0=g2[:], in1=lnb[:], op=ALU.add),
                                ], reads=[st, g1, g2, lng, lnb], writes=[g2, so])
                                sc.dma(pool, VB[seg][r0:r0 + 128, :], so[:], so, reads=[so], writes=[sc.dbuf(("vb", seg, tl))])
                                dump_stat(st)
                            elif p == 2:
                                vc_ = vcs[i]
                                vv = vc_[:].rearrange("p (h d) -> p h d", h=12)
                                sc.op(dve, lambda e, vv=vv, bk=bk, vcol=vcol: e.tensor_scalar(
                                    out=vv[:, 0:8, 0:64], in0=psf(bk).rearrange("p (h d) -> p h d", h=8),
                                    scalar1=vcol, scalar2=None, op0=ALU.mult),
                                    reads=[PS[bk], validt], writes=[vc_])
                            else:
                                vc_ = vcs[i]
                                va_ = vas[i]
                                vv = vc_[:].rearrange("p (h d) -> p h d", h=12)
                                av = va_[:].rearrange("p (h d) -> p h d", h=4)
                                pv_ = psf(bk).rearrange("p (h d) -> p h d", h=8)
                                vb1 = bass.AP(validt.t[:].tensor, vcol.offset, [list(vcol.ap[0]), [0, 12], [1, 1]])
                                vb2 = bass.AP(validt.t[:].tensor, vcol.offset, [list(vcol.ap[0]), [0, 4], [1, 1]])
                                sc.op(dve, [
                                    lambda e, vv=vv, pv_=pv_, vcol=vcol: e.tensor_scalar(out=vv[:, 8:12, 0:64], in0=pv_[:, 0:4, :],
                                                                                      scalar1=vcol, scalar2=None, op0=ALU.mult),
                                    lambda e, av=av, pv_=pv_, vcol=vcol: e.tensor_scalar(out=av[:, :, 0:64], in0=pv_[:, 4:8, :],
                                                                                      scalar1=vcol, scalar2=None, op0=ALU.mult),
                                    lambda e, vv=vv, vb1=vb1: e.tensor_copy(out=vv[:, :, 64:65], in_=vb1),
                                    lambda e, av=av, vb2=vb2: e.tensor_copy(out=av[:, :, 64:65], in_=vb2),
                                ], reads=[PS[bk], validt], writes=[vc_, va_])
                                sc.dma(pool, VCd[seg][r0:r0 + 128, :], vc_[:], vc_, reads=[vc_], writes=[sc.dbuf(("vc", seg, tl))])
                                sc.dma(pool, VAd[seg][r0:r0 + 128, :], va_[:], va_, reads=[va_], writes=[sc.dbuf(("va", seg, tl))])
                    gi += 1
                sc.barrier()
                sc.release(allb)
            if _STOP[0] == 'P':
                return nc

            with ExitStack() as es:
                TA = sb(es, "TA", [128, 12 * 384], BF16)
                TBw = sb(es, "TBw", [128, 84 * 128], BF16)
                TBe = sb(es, "TBe", [128, 24 * 128], BF16)
                qa = [sb(es, "qa%d" % i, [128, 2 * 6 * 128], BF16) for i in range(2)]
                ka = [sb(es, "ka%d" % i, [128, 2 * 384], BF16) for i in range(2)]
                va = [sb(es, "va%d" % i, [128, 3 * 260], BF16) for i in range(2)]
                qc = sb(es, "qc", [128, 2 * 6 * 128], BF16)
                kc = sb(es, "kc", [128, 6 * 896], BF16)
                vc = sb(es, "vc", [128, 7 * 780], BF16)
                gu = sb(es, "gu", [128, 512], F32)
                vn = sb(es, "vn", [128, 512], BF16)
                PT = [sb(es, "PT%d" % i, [128, 1024], BF16) for i in range(2)]
                ycat = sb(es, "ycat", [128, D], F32)
                mrg = sb(es, "mrg", [128, D], BF16)
                mT = sb(es, "mT", [128, 16 * 512], BF16)
                wo = [sb(es, "wo%d" % i, [128, 8192], BF16) for i in range(2)]
                hq = [sb(es, "hq%d" % i, [128, D], F32) for i in range(4)]
                wsT = sb(es, "wsT", [128, 8 * 128], BF16)
                bsb = sb(es, "bsb", [128, 8], F32)
                esk = sb(es, "esk", [128, 12], F32)
                den = sb(es, "den", [128, 24], F32)
                allb = [TA, TBw, TBe, qc, kc, vc, gu, vn, ycat, mrg, mT, wsT, bsb, esk, den] + PT + wo + hq + qa + ka + va
                tmpf, tmpm = hq[0], hq[1]
                tcv = tabC_d[l].rearrange("p (h c q) -> p h c q", h=12, c=7)
                jobs = [(tabA_d[:, c0:min(c0 + 2048, 4608)], mA_d[:, c0:min(c0 + 2048, 4608)], TA, c0, min(2048, 4608 - c0)) for c0 in range(0, 4608, 2048)]
                jobs += [(tabC_d[l][:, c0:min(c0 + 2048, 10752)], mC_d[:, c0:min(c0 + 2048, 10752)], TBw, c0, min(2048, 10752 - c0)) for c0 in range(0, 10752, 2048)]
                for (tab_ap, m_ap, dst, c0, cn) in jobs:
                    sc.dma(sp, tmpf[:, 0:cn], tab_ap[:, 0:cn], tmpf, writes=[tmpf])
                    sc.dma(sp, tmpm[:, 0:cn], m_ap[:, 0:cn], tmpm, writes=[tmpm])
                    sc.op(dve, [lambda e, cn=cn: e.tensor_scalar(out=tmpm[:, 0:cn], in0=tmpm[:, 0:cn], scalar1=-1.0, scalar2=1e30,
                                                                 op0=ALU.add, op1=ALU.mult),
                                lambda e, cn=cn, c0=c0, dst=dst: e.scalar_tensor_tensor(out=dst[:, c0:c0 + cn], in0=tmpf[:, 0:cn], scalar=8.0,
                                                                                      in1=tmpm[:, 0:cn], op0=ALU.mult, op1=ALU.add)],
                          reads=[tmpf, tmpm], writes=[tmpm, dst])
                for hh in range(0, 12, 6):
                    tv_ = tmpf[:, 0:1536].rearrange("p (h c q) -> p h c q", h=6, c=2)
                    for ci_, cw_ in ((0, 1), (1, 5)):
                        sc.dma(sp, tv_[:, :, ci_, :], tcv[:, hh:hh + 6, cw_, :], tmpf, writes=[tmpf])
                    sc.dma(sp, tmpm[:, 0:1536], mE_d[:, hh * 256:hh * 256 + 1536], tmpm, writes=[tmpm])
                    sc.op(dve, [lambda e: e.tensor_scalar(out=tmpm[:, 0:1536], in0=tmpm[:, 0:1536], scalar1=-1.0, scalar2=1e30,
                                                          op0=ALU.add, op1=ALU.mult),
                                lambda e, hh=hh: e.scalar_tensor_tensor(out=TBe[:, hh * 256:hh * 256 + 1536], in0=tmpf[:, 0:1536], scalar=8.0,
                                                                        in1=tmpm[:, 0:1536], op0=ALU.mult, op1=ALU.add)],
                          reads=[tmpf, tmpm], writes=[tmpm, TBe])
                sc.op(dve, [lambda e: e.memset(qa[0][:], 0.0), lambda e: e.memset(qa[1][:], 0.0), lambda e: e.memset(qc[:], 0.0)],
                      writes=[qa[0], qa[1], qc])
                sc.dma(sp, hq[2][:, 0:1024], wsT_d[l], hq[2], writes=[hq[2]])
                sc.op(dve, lambda e: e.tensor_copy(out=wsT[:], in_=hq[2][:, 0:1024]), reads=[hq[2]], writes=[wsT])
                sc.dma(sp, bsb[:], bs_d[l], bsb, writes=[bsb])
                sc.dma(sp, esk[:], sink_d[l], esk, writes=[esk])
                sc.op(act, lambda e: e.activation(out=esk[:], in_=esk[:], func=AF.Exp), reads=[esk], writes=[esk])

                mtile = (m1 - m0) // 128
                pj0, pj1 = p0 // 128, p1 // 128
                specials = (H // 128, H // 128 + 1, (H + T) // 128 - 2, (H + T) // 128 - 1)
                nwo = 0
                nbat = 0
                for gt in range(0, mtile, 4):
                    ng = min(4, mtile - gt)
                    for i in range(ng):
                        j = m0 // 128 + gt + i
                        t0 = j * 128
                        jp = j % 2
                        qa_, ka_, va_ = qa[jp], ka[jp], va[jp]
                        special = j in specials
                        for (qb_, fc0) in ((qa_, 0), (qc, 8)):
                            qv_ = qb_[:].rearrange("p (z c t) -> p z c t", z=2, c=6)
                            for hf in range(2):
                                sc.dma(sp, qv_[hf * 64:(hf + 1) * 64, hf, :, :],
                                       featT[seg][fc0:fc0 + 6, hf * 64:(hf + 1) * 64, t0:t0 + 128].rearrange("c p t -> p c t"),
                                       qb_, reads=[sc.dbuf(("f", seg, j))], writes=[qb_])
                        ca = [c for c in range(3) if pj0 <= j + c - 1 < pj1]
                        cc_ = [c for c in (range(7) if special else range(1, 6)) if pj0 <= j + c - 3 < pj1]
                        ja0, ja1 = j + ca[0] - 1, j + ca[-1] - 1
                        jc0, jc1 = j + cc_[0] - 3, j + cc_[-1] - 3
                        sc.dma(sp, ka_[:].rearrange("p (c t) -> p c t", c=2)[:, :, ca[0] * 128:(ca[-1] + 1) * 128],
                               featT[seg][6:8, :, ja0 * 128:(ja1 + 1) * 128].rearrange("c p t -> p c t"), ka_,
                               reads=[sc.dbuf(("f", seg, x)) for x in range(ja0, ja1 + 1)], writes=[ka_])
                        sc.dma(sp, va_[:].rearrange("p (c f) -> p c f", c=3)[:, ca[0]:ca[-1] + 1, :],
                               VAd[seg][ja0 * 128:(ja1 + 1) * 128, :].rearrange("(c p) f -> p c f", p=128), va_,
                               reads=[sc.dbuf(("va", seg, x)) for x in range(ja0, ja1 + 1)], writes=[va_])
                        sc.dma(sp, kc[:].rearrange("p (c t) -> p c t", c=6)[:, :, cc_[0] * 128:(cc_[-1] + 1) * 128],
                               featT[seg][14:20, :, jc0 * 128:(jc1 + 1) * 128].rearrange("c p t -> p c t"), kc,
                               reads=[sc.dbuf(("f", seg, x)) for x in range(jc0, jc1 + 1)], writes=[kc])
                        sc.dma(sp, vc[:].rearrange("p (c f) -> p c f", c=7)[:, cc_[0]:cc_[-1] + 1, :],
                               VCd[seg][jc0 * 128:(jc1 + 1) * 128, :].rearrange("(c p) f -> p c f", p=128), vc,
                               reads=[sc.dbuf(("vc", seg, x)) for x in range(jc0, jc1 + 1)], writes=[vc])
                        sc.dma(sp, gu[:], UB[seg][t0:t0 + 128, :], gu, reads=[sc.dbuf(("ub", seg, j))], writes=[gu])
                        sc.dma(sp, vn[:], VB[seg][t0:t0 + 128, :], vn, reads=[sc.dbuf(("vb", seg, j))], writes=[vn])
                        qav = qa_[:].rearrange("p (z c t) -> p z c t", z=2, c=6)
                        kav = ka_[:].rearrange("p (c t) -> p c t", c=2)
                        vav = va_[:].rearrange("p (c f) -> p c f", c=3)
                        qcv = qc[:].rearrange("p (z c t) -> p z c t", z=2, c=6)
                        kcv = kc[:].rearrange("p (c t) -> p c t", c=6)
                        vcv = vc[:].rearrange("p (c f) -> p c f", c=7)

                        def ocol(h):
                            return (4, h * 65) if h < 7 else (5, (h - 7) * 65)

                        def finish_heads(dst0, sink):
                            d7 = den[:, 0:7]
                            d5 = den[:, 7:12]
                            o7 = psf(4, 455).rearrange("p (h d) -> p h d", h=7)
                            o5 = psf(5, 325).rearrange("p (h d) -> p h d", h=5)
                            if sink:
                                fns = [lambda e: e.tensor_tensor(out=d7.unsqueeze(2), in0=o7[:, :, 64:65], in1=esk[:, 0:7].unsqueeze(2), op=ALU.add),
                                       lambda e: e.tensor_tensor(out=d5.unsqueeze(2), in0=o5[:, :, 64:65], in1=esk[:, 7:12].unsqueeze(2), op=ALU.add)]
                            else:
                                fns = [lambda e: e.tensor_scalar(out=d7.unsqueeze(2), in0=o7[:, :, 64:65], scalar1=1e-30, scalar2=None, op0=ALU.max),
                                       lambda e: e.tensor_scalar(out=d5.unsqueeze(2), in0=o5[:, :, 64:65], scalar1=1e-30, scalar2=None, op0=ALU.max)]
                            sc.op(dve, fns, reads=PS[4:6] + [esk], writes=[den], tiny=True)
                            sc.op(dve, lambda e: e.reciprocal(out=den[:, 12:24], in_=den[:, 0:12]), reads=[den], writes=[den], tiny=True)
                            r7 = bass.AP(den.t[:].tensor, den[:, 12:19].offset, [list(den[:, 12:19].ap[0]), [1, 7], [0, 64]])
                            r5 = bass.AP(den.t[:].tensor, den[:, 19:24].offset, [list(den[:, 19:24].ap[0]), [1, 5], [0, 64]])
                            y7 = ycat[:, dst0:dst0 + 448].rearrange("p (h d) -> p h d", h=7)
                            y5 = ycat[:, dst0 + 448:dst0 + 768].rearrange("p (h d) -> p h d", h=5)
                            sc.op(dve, [lambda e: e.tensor_tensor(out=y7, in0=o7[:, :, 0:64], in1=r7, op=ALU.mult),
                                        lambda e: e.tensor_tensor(out=y5, in0=o5[:, :, 0:64], in1=r5, op=ALU.mult)],
                                  reads=PS[4:6] + [den], writes=[ycat])

                        batches = []
                        for b in range(6):
                            qk, pvf, spans = [], [], []
                            for k in range(2):
                                pi = 2 * b + k
                                half, qch = pi % 2, pi // 2
                                h = QA_ORDER[pi]
                                kvh = h // 3
                                kch = kvh // 2
                                bk, oc = ocol(h)
                                for c in ca:
                                    col = k * 512 + c * 128
                                    qk.append((col, kav[:, kch, c * 128:(c + 1) * 128], qav[:, half, qch, :], TA[:, (pi * 3 + c) * 128:(pi * 3 + c + 1) * 128]))
                                    pvf.append((bk, oc, col, vav[:, c, kvh * 65:(kvh + 1) * 65], c == ca[0], c == ca[-1]))
                                spans.append((k * 512 + ca[0] * 128, k * 512 + (ca[-1] + 1) * 128))
                            batches.append(dict(qk=qk, pv=pvf, spans=spans, rq=[qa_, ka_, TA, ident], rv=[va_], mask=False))
                        for h in range(12):
                            b, k = h // 2, h % 2
                            bk, oc = ocol(h)
                            qk, pvf = [], []
                            for c in cc_:
                                col = c * 128
                                if special or c not in (1, 5):
                                    tb_ = TBw[:, (h * 7 + c) * 128:(h * 7 + c + 1) * 128]
                                else:
                                    e_ = h * 2 + (0 if c == 1 else 1)
                                    tb_ = TBe[:, e_ * 128:(e_ + 1) * 128]
                                qk.append((col, kcv[:, b, c * 128:(c + 1) * 128], qcv[:, k, b, :], tb_))
                                pvf.append((bk, oc, col, vcv[:, c, h * 65:(h + 1) * 65], c == cc_[0], c == cc_[-1]))
                            batches.append(dict(qk=qk, pv=pvf, spans=[(cc_[0] * 128, (cc_[-1] + 1) * 128)], rq=[qc, kc, TBw, TBe, ident], rv=[vc],
                                                mask=special))

                        def emit_qk(bt, bn):
                            base = (bn % 2) * 1024
                            fns = []
                            for (col, kk, qq, tb_) in bt["qk"]:
                                o_ = psum[:, base + col: base + col + 128]
                                fns.append(lambda e, o_=o_, kk=kk, qq=qq: e.matmul(o_, kk, qq, start=True, stop=False))
                                fns.append(lambda e, o_=o_, tb_=tb_: e.matmul(o_, ident[:], tb_, start=False, stop=True))
                            sc.op(pe, fns, reads=bt["rq"], writes=PS[(bn % 2) * 2:(bn % 2) * 2 + 2])

                        def emit_rest(bt, bn):
                            base = (bn % 2) * 1024
                            pt = PT[bn % 2]
                            sc.op(act, [lambda e, a_=a_, b_=b_, pt=pt: e.activation(out=pt[:, a_:b_], in_=psum[:, base + a_: base + b_], func=AF.Exp,
                                                                                  scale=0.125) for (a_, b_) in bt["spans"]],
                                  reads=PS[(bn % 2) * 2:(bn % 2) * 2 + 2], writes=[pt])
                            if bt["mask"]:
                                rmo = (tile_base[seg] + j) * 14
                                rmv = bass.AP(rowm.t[:].tensor, rowm[:, rmo:rmo + 14].offset,
                                              [list(rowm[:, rmo:rmo + 14].ap[0]), [1, 14], [0, 64]])
                                pv_ = pt[:, 0:896].rearrange("p (c q) -> p c q", c=14)
                                sc.op(pool, lambda e, pv_=pv_, rmv=rmv: e.tensor_tensor(out=pv_, in0=pv_, in1=rmv, op=ALU.mult),
                                      reads=[pt, rowm], writes=[pt])
                            sc.op(pe, [lambda e, bk=bk, oc=oc, col=col, vv=vv, s_=s_, t_=t_, pt=pt: e.matmul(
                                psf(bk, 65, oc), pt[:, col:col + 128], vv, start=s_, stop=t_) for (bk, oc, col, vv, s_, t_) in bt["pv"]],
                                reads=[pt] + bt["rv"], writes=PS[4:6])

                        def emit_B():
                            sc.op(pe, [lambda e, g=g: e.matmul(psf(6, 64, g * 64), wsT[:, g * 128:(g + 1) * 128], vn[:, g * 64:(g + 1) * 64],
                                                               start=True, stop=True) for g in range(8)],
                                  reads=[wsT, vn], writes=[PS[6]])
                            sc.op(dve, [lambda e, g=g: e.scalar_tensor_tensor(out=ycat[:, 768 + g * 64:768 + (g + 1) * 64], in0=psf(6, 64, g * 64),
                                                                              scalar=bsb[:, g:g + 1], in1=gu[:, g * 64:(g + 1) * 64],
                                                                              op0=ALU.add, op1=ALU.mult) for g in range(8)],
                                  reads=[PS[6], bsb, gu], writes=[ycat])

                        def emit_gnorm(c0, cn):
                            st = stat[statn[0] % 4]
                            sc.op(act, lambda e, st=st: e.activation(out=mrg[:, c0:c0 + cn], in_=ycat[:, c0:c0 + cn], func=AF.Square,
                                                                     accum_out=st[:, 0:1]),
                                  reads=[ycat], writes=[mrg, st])
                            st2, r = rstd_from_ss(st[:, 0:1], st, cn)
                            sc.op(dve, lambda e, r=r: e.tensor_scalar(out=mrg[:, c0:c0 + cn], in0=ycat[:, c0:c0 + cn], scalar1=r,
                                                                      scalar2=None, op0=ALU.mult),
                                  reads=[ycat, st2], writes=[mrg])

                        emit_qk(batches[0], nbat)
                        for bi, bt in enumerate(batches):
                            if bi + 1 < len(batches):
                                emit_qk(batches[bi + 1], nbat + bi + 1)
                            emit_rest(bt, nbat + bi)
                            if bi == 5:
                                finish_heads(0, True)
                                emit_B()
                            if bi == 8:
                                emit_gnorm(0, 768)
                                emit_gnorm(768, 512)
                        finish_heads(1280, False)
                        nbat += len(batches)
                        emit_gnorm(1280, 768)
                        transpose_tile(mrg, mT, i * 128, (6, 7), ev=i)
                    mv = mT[:].rearrange("p (k t) -> p k t", k=16)
                    for dg in range(4):
                        w_ = wo[nwo % 2]
                        nwo += 1
                        wpiece_load(w_, l, P_OUT + dg)
                        wv = w_[:].rearrange("p (k c) -> p k c", k=16)
                        for i in range(ng):
                            r0 = m0 + (gt + i) * 128
                            bk = 6 + (dg * ng + i) % 2
                            sc.op(pe, [lambda e, k=k, i=i, bk=bk, wv=wv: e.matmul(psf(bk), mv[:, k, i * 128:(i + 1) * 128], wv[:, k, :],
                                                                               start=(k == 0), stop=(k == 15)) for k in range(16)],
                                  reads=[mT, w_], writes=[PS[bk]])
                            hr = hq[i]
                            if dg == 0:
                                sc.dma(sp, hr[:], hsrc[r0:r0 + 128, :], hr, reads=rows_deps(hkind, seg, r0, r0 + 128), writes=[hr])
                            sc.op(dve, lambda e, hr=hr, bk=bk, dg=dg: e.tensor_tensor(out=hr[:, dg * 512:(dg + 1) * 512], in0=psf(bk),
                                                                                   in1=hr[:, dg * 512:(dg + 1) * 512], op=ALU.add),
                                  reads=[PS[bk], hr], writes=[hr])
                            if dg == 3:
                                sc.dma(pool, hmid[seg][r0:r0 + 128, :], hr[:], hr, reads=[hr], writes=[sc.dbuf(("m", seg, r0 // 128))])
                sc.barrier()
                sc.release(allb)
            if _STOP[0] == 'M':
                return nc

            with ExitStack() as es:
                hb = [sb(es, "fhb%d" % i, [128, D], F32) for i in range(2)]
                xs = [sb(es, "fxs%d" % i, [128, D], BF16) for i in range(2)]
                xT2 = sb(es, "xT2", [128, 16 * 512], BF16)
                AT = sb(es, "AT", [128, 43 * 512], BF16)
                wb = [sb(es, "fwb%d" % i, [128, 8192], BF16) for i in range(2)]
                tg = [sb(es, "tg%d" % i, [128, 512], F32) for i in range(2)]
                tv = [sb(es, "tv%d" % i, [128, 512], F32) for i in range(2)]
                hout = [sb(es, "hout%d" % i, [128, D], F32) for i in range(4)]
                cwb = sb(es, "cwb", [128, NFC * 4], F32)
                allb = hb + xs + [xT2, AT, cwb] + wb + tg + tv + hout
                sc.dma(sp, cwb[:], cw_d[l], cwb, writes=[cwb])
                if last:
                    fgb = sb(es, "fgb", [128, D], F32)
                    allb.append(fgb)
                    sc.dma(sp, fgb[:], fg_d[:, :], fgb, writes=[fgb])
                else:
                    ccin = [sb(es, "ccin%d" % i, [128, 2048], F32) for i in range(2)]
                    ccout = [sb(es, "ccout%d" % i, [128, 2048], BF16) for i in range(2)]
                    cgc = sb(es, "cgc", [128, 48 * depth], F32)
                    allb += ccin + ccout + [cgc]
                    for l_ in range(depth):
                        sc.dma(sp, cgc[:, 48 * l_:48 * l_ + 48], gcol_d[l_], cgc, writes=[cgc])
                    if seg == 0:
                        caster.start_layer(l + 1)
                    caster.bufs = (ccin, ccout, cgc)
                xv = xT2[:].rearrange("p (k t) -> p k t", k=16)
                starts = list(range(f0, f1, 510))
                nw = 0
                for t0 in starts:
                    w0 = t0 - 1
                    n_ = min(510, f1 - t0)
                    N = n_ + 2
                    nst_ = (n_ + 127) // 128
                    for i in range((N + 127) // 128):
                        r0 = w0 + 128 * i
                        norm_tile(hmid[seg][r0:r0 + 128, :], rows_deps("m", seg, r0, r0 + 128), hb[i % 2], xs[i % 2])
                        transpose_tile(xs[i % 2], xT2, i * 128, (0, 1), ev=i)
                    for (tokpos, fcol) in ((H - 1, 0), (H + T, 1)):
                        cL = tokpos - w0
                        if 0 <= cL < N:
                            sc.op(dve, lambda e, cL=cL, fcol=fcol: e.tensor_scalar(out=xv[:, :, cL:cL + 1], in0=xv[:, :, cL:cL + 1],
                                                                                 scalar1=flags[:, fcol:fcol + 1], scalar2=None, op0=ALU.mult),
                                  reads=[xT2, flags], writes=[xT2])
                    for st_ in range(nst_):
                        M = min(128, n_ - 128 * st_)
                        r0 = t0 + 128 * st_
                        sc.dma(sp, hout[st_][0:M, :], hmid[seg][r0:r0 + M, :], hout[st_], reads=rows_deps("m", seg, r0, r0 + M),
                               writes=[hout[st_]])
                    for p in range(22):
                        if p % 2 == 1:
                            caster.point()
                        w_ = wb[nw % 2]
                        nw += 1
                        wpiece_load(w_, l, P_UP + p)
                        wv = w_[:].rearrange("p (k c) -> p k c", k=16)
                        for cc in range(4):
                            f = 4 * p + cc
                            if f >= NFC:
                                break
                            bk = 2 + f % 2
                            sc.op(pe, [lambda e, k=k, cc=cc, bk=bk, wv=wv: e.matmul(psf(bk, N), wv[:, k, cc * 128:(cc + 1) * 128], xv[:, k, 0:N],
                                                                                 start=(k == 0), stop=(k == 15)) for k in range(16)],
                                  reads=[w_, xT2], writes=[PS[bk]])
                            pi = f // 2
                            gate = (f % 2 == 0)
                            tb = (tg if gate else tv)[pi % 2]
                            cwv = cwb[:, f * 4:(f + 1) * 4]
                            sc.op(act, lambda e, tb=tb, bk=bk, cwv=cwv: e.activation(out=tb[:, 0:n_], in_=psf(bk, n_, 1), func=AF.Identity,
                                                                                  bias=cwv[:, 3:4], scale=cwv[:, 1:2]),
                                  reads=[PS[bk], cwb], writes=[tb])
                            sc.op(dve, [
                                lambda e, tb=tb, bk=bk, cwv=cwv: e.scalar_tensor_tensor(out=tb[:, 0:n_], in0=psf(bk, n_, 0), scalar=cwv[:, 0:1],
                                                                                     in1=tb[:, 0:n_], op0=ALU.mult, op1=ALU.add),
                                lambda e, tb=tb, bk=bk, cwv=cwv: e.scalar_tensor_tensor(out=tb[:, 0:n_], in0=psf(bk, n_, 2), scalar=cwv[:, 2:3],
                                                                                     in1=tb[:, 0:n_], op0=ALU.mult, op1=ALU.add),
                            ], reads=[PS[bk], cwb, tb], writes=[tb])
                            if gate:
                                sc.op(act, lambda e, tb=tb: e.activation(out=tb[:, 0:n_], in_=tb[:, 0:n_], func=AF.Gelu_apprx_tanh),
                                      reads=[tb], writes=[tb])
                            else:
                                tgb = tg[pi % 2]
                                sc.op(pool, lambda e, tb=tb, tgb=tgb, pi=pi: e.tensor_tensor(out=AT[:, pi * 512 + 1:pi * 512 + 1 + n_], in0=tgb[:, 0:n_],
                                                                                         in1=tb[:, 0:n_], op=ALU.mult),
                                      reads=[tb, tgb], writes=[AT])
                    for dg in range(4):
                        caster.point()
                        for pc in range(3):
                            w_ = wb[nw % 2]
                            nw += 1
                            wpiece_load(w_, l, P_DN + dg * 3 + pc)
                            wv = w_[:].rearrange("p (k c) -> p k c", k=16)
                            fns = []
                            for fi in range(16):
                                f = pc * 16 + fi
                                if f >= 43:
                                    break
                                for st_ in range(nst_):
                                    M = min(128, n_ - 128 * st_)
                                    c0 = f * 512 + 1 + 128 * st_
                                    fns.append(lambda e, fi=fi, f=f, st_=st_, M=M, c0=c0, wv=wv: e.matmul(
                                        psum[0:M, (4 + st_) * 512:(5 + st_) * 512], AT[:, c0:c0 + M], wv[:, fi, :],
                                        start=(f == 0), stop=(f == 42)))
                            sc.op(pe, fns, reads=[AT, w_], writes=PS[4:8])
                        for st_ in range(nst_):
                            M = min(128, n_ - 128 * st_)
                            r0 = t0 + 128 * st_
                            ho = hout[st_]
                            sc.op(dve, lambda e, ho=ho, st_=st_, M=M, dg=dg: e.tensor_tensor(
                                out=ho[0:M, dg * 512:(dg + 1) * 512], in0=psum[0:M, (4 + st_) * 512:(5 + st_) * 512],
                                in1=ho[0:M, dg * 512:(dg + 1) * 512], op=ALU.add),
                                reads=[PS[4 + st_], ho], writes=[ho])
                            if dg == 3:
                                if not last:
                                    sc.dma(pool, hbuf[seg][r0:r0 + M, :], ho[0:M, :], ho, reads=[ho], writes=rows_deps("h", seg, r0, r0 + M))
                                else:
                                    st = stat[statn[0] % 4]
                                    xs_ = xs[st_ % 2]
                                    sc.op(act, lambda e, ho=ho, M=M, st=st, xs_=xs_: e.activation(out=xs_[0:M, :], in_=ho[0:M, :], func=AF.Square,
                                                                                             accum_out=st[0:M, 0:1]),
                                          reads=[ho], writes=[xs_, st])
                                    st2, r = rstd_from_ss(st[:, 0:1], st, D)
                                    sc.op(dve, [
                                        lambda e, ho=ho, M=M, st=st: e.tensor_scalar(out=ho[0:M, :], in0=ho[0:M, :], scalar1=st[0:M, 2:3], scalar2=None,
                                                                                   op0=ALU.mult),
                                        lambda e, ho=ho, M=M: e.tensor_tensor(out=ho[0:M, :], in0=ho[0:M, :], in1=fgb[0:M, :], op=ALU.mult),
                                    ], reads=[ho, st2, fgb], writes=[ho])
                                    sc.dma(pool, yout[seg][r0 - H:r0 - H + M, :], ho[0:M, :], ho, reads=[ho], writes=[sc.dbuf(("y", seg, r0))])
                if seg == nseg - 1:
                    caster.flush()
                caster.drain()
                caster.bufs = None
                sc.barrier()
                sc.release(allb)
    sc.barrier()
    return nc


_CACHE = {}


def _run(inp, depth, seqs):
    inp = {k: np.asarray(v) for k, v in inp.items()}
    xs_full = [inp["x_prompt"], inp["x_sample"]]
    Ts = [s // 4 for s in seqs]
    H = HALO_L * depth
    key = (depth, tuple(Ts))
    if key not in _CACHE:
        _CACHE[key] = build(depth, Ts)
    nc = _CACHE[key]
    W, gcol, cw = _host_weights(inp, depth)
    tabA, mA, tabC, mC, mE = _host_tables(inp, depth)
    ident = np.eye(128, dtype=np.float32).astype(ml_dtypes.bfloat16)
    wsT = np.ascontiguousarray(inp["w_spatial"][:depth].transpose(0, 3, 1, 2)).reshape(depth, 128, 1024).astype(np.float32)
    bs = np.ascontiguousarray(inp["b_spatial"][:depth].transpose(0, 2, 1)).astype(np.float32)
    lng = np.ascontiguousarray(np.broadcast_to(inp["sgu_ln_g"][:depth, None, :], (depth, 128, 512))).astype(np.float32)
    lnb = np.ascontiguousarray(np.broadcast_to(inp["sgu_ln_b"][:depth, None, :], (depth, 128, 512))).astype(np.float32)
    sink = np.ascontiguousarray(np.broadcast_to(inp["sink"][:depth, None, :], (depth, 128, 12))).astype(np.float32)
    fg = np.ascontiguousarray(np.broadcast_to(inp["final_g"][None, :], (128, D))).astype(np.float32)
    shared = dict(wf32=W, gcol=gcol, cw=cw, tabA=tabA, mA=mA, tabC=tabC, mC=mC, mE=mE, ident=ident, wsT=wsT, bs=bs, lng=lng,
                  lnb=lnb, sink=sink, fg=fg)
    in_maps = []
    for c in range(8):
        b, q = c // 4, c % 4
        m = dict(shared)
        rowms, valids = [], []
        for s_, (xf, T, seq) in enumerate(zip(xs_full, Ts, seqs)):
            E = T + 2 * H
            lo = q * T - H
            xe = np.zeros((E, D), np.float32)
            a0, a1 = max(lo, 0), min(lo + E, seq)
            xe[a0 - lo:a1 - lo] = xf[b, a0:a1]
            m["x%d" % s_] = xe
            pos = lo + np.arange(E)
            v = ((pos >= 0) & (pos < seq)).astype(np.float32)
            valids.append(v.reshape(E // 128, 128).T)
            rowms.append(_na_rowmask(seq // 64, lo // 64, E // 128))
        m["valid"] = np.ascontiguousarray(np.concatenate(valids, 1))
        m["rowm"] = np.ascontiguousarray(np.concatenate(rowms, 1))
        fl = np.zeros((128, 2), np.float32)
        fl[:, 0] = 1.0 if q > 0 else 0.0
        fl[:, 1] = 1.0 if q < 3 else 0.0
        m["flags"] = fl
        in_maps.append(m)
    res = run_bass_kernel_spmd(nc, in_maps, core_ids=list(range(8)))
    if _DBG:
        return res.results
    outs = [np.zeros((2, seqs[0], D), np.float32), np.zeros((2, seqs[1], D), np.float32)]
    for c in range(8):
        b, q = c // 4, c % 4
        for s_ in range(2):
            outs[s_][b, q * Ts[s_]:(q + 1) * Ts[s_]] = res.results[c]["y%d" % s_]
    return tuple(outs)


def kernel(**inputs):
    return _run(inputs, 4, (16384, 8192))
```

```python
import math
from contextlib import ExitStack
import numpy as np
import ml_dtypes
import concourse.bass as bass
import concourse.mybir as mybir
from concourse.bass_utils import run_bass_kernel_spmd

F32 = mybir.dt.float32
BF16 = mybir.dt.bfloat16
AF = mybir.ActivationFunctionType
ALU = mybir.AluOpType

D = 2048
KC = 16
DFF = 5504
NFC = 86
HD = 64
EPS = 1e-6
HALO_L = 384
QA_ORDER = [0, 3, 1, 4, 2, 5, 6, 9, 7, 10, 8, 11]
NPIECE = 47
P_IN, P_OUT, P_UP, P_DN = 0, 9, 13, 35


class Tok:
    __slots__ = ("sem", "val", "eng", "tiny")

    def __init__(self, sem, val, eng, tiny=False):
        self.sem, self.val, self.eng, self.tiny = sem, val, eng, tiny


class DSem:
    def __init__(self, h):
        self.h, self.n = h, 0


class Buf:
    def __init__(self, name, t=None):
        self.name, self.t = name, t
        self.w = None
        self.r = []
        self.dsem = None

    def __getitem__(self, k):
        return self.t[k]


class Eng:
    def __init__(self, name, e, sem, compute=True):
        self.name, self.e, self.sem, self.compute = name, e, sem, compute
        self.n = 0
        self.seen = {}

    def wait(self, tok):
        if tok is None:
            return
        if self.compute and tok.eng is self and not tok.tiny:
            return
        k = id(tok.sem)
        if self.seen.get(k, 0) >= tok.val:
            return
        self.e.wait_ge(tok.sem, tok.val)
        self.seen[k] = tok.val


class S:
    def __init__(self, nc):
        self.nc = nc
        self.pe = Eng("pe", nc.tensor, nc.alloc_semaphore("s_pe"))
        self.act = Eng("act", nc.scalar, nc.alloc_semaphore("s_act"))
        self.dve = Eng("dve", nc.vector, nc.alloc_semaphore("s_dve"))
        self.pool = Eng("pool", nc.gpsimd, nc.alloc_semaphore("s_pool"))
        self.sp = Eng("sp", nc.sync, None, compute=False)
        self.engs = [self.pe, self.act, self.dve, self.pool, self.sp]
        self.free_ds = []
        self.all_ds = []
        self.dram = {}

    def dbuf(self, key):
        b = self.dram.get(key)
        if b is None:
            b = Buf(str(key))
            self.dram[key] = b
        return b

    def _deps(self, reads, writes):
        toks = []
        for b in reads:
            if b.w is not None:
                toks.append(b.w)
        for b in writes:
            if b.w is not None:
                toks.append(b.w)
            toks.extend(b.r)
        return toks

    def _commit(self, tok, reads, writes):
        for b in reads:
            b.r.append(tok)
            if len(b.r) > 24:
                b.r = b.r[-24:]
        for b in writes:
            b.w = tok
            b.r = []

    def op(self, eng, fns, reads=(), writes=(), tiny=False):
        if callable(fns):
            fns = [fns]
        for t in self._deps(reads, writes):
            eng.wait(t)
        ins = None
        for f in fns:
            ins = f(eng.e)
        eng.n += 1
        ins.then_inc(eng.sem, 1)
        tok = Tok(eng.sem, eng.n, eng, tiny)
        self._commit(tok, reads, writes)
        return tok

    def _getds(self, b):
        if b.dsem is None:
            if self.free_ds:
                b.dsem = self.free_ds.pop()
            else:
                b.dsem = DSem(self.nc.alloc_semaphore("d%d" % len(self.all_ds)))
                self.all_ds.append(b.dsem)
        return b.dsem

    def dma(self, q, out, in_, sb, reads=(), writes=()):
        for t in self._deps(reads, writes):
            q.wait(t)
        ds = self._getds(sb)
        ds.n += 1
        q.e.dma_start(out=out, in_=in_).then_inc(ds.h, 16)
        tok = Tok(ds.h, 16 * ds.n, None)
        self._commit(tok, reads, writes)
        return tok

    def release(self, bufs):
        for b in bufs:
            if b.dsem is not None:
                self.free_ds.append(b.dsem)
                b.dsem = None

    def barrier(self):
        toks = [Tok(d.h, 16 * d.n, None) for d in self.all_ds if d.n > 0]
        toks += [Tok(e.sem, e.n, e) for e in self.engs if e.compute and e.n > 0]
        for e in self.engs:
            for t in toks:
                e.wait(t)


def _t5_bucket(rel):
    nb = 16
    max_exact = 8
    ret = (rel > 0).astype(np.int64) * nb
    n = np.abs(rel)
    nf = np.maximum(n, 1).astype(np.float32)
    large = max_exact + (np.log(nf / max_exact) / math.log(128 / max_exact) * (nb - max_exact)).astype(np.int64)
    large = np.minimum(large, nb - 1)
    return ret + np.where(n < max_exact, n, large)


def _bf16(a):
    return np.ascontiguousarray(a).astype(ml_dtypes.bfloat16)


def _host_weights(inp, depth):
    cols_feat = []
    for i in range(6):
        a, b = QA_ORDER[2 * i], QA_ORDER[2 * i + 1]
        cols_feat += list(range(a * 64, a * 64 + 64)) + list(range(b * 64, b * 64 + 64))
    cols_feat += list(range(768, 1024))
    cols_feat += list(range(2304, 3072))
    cols_feat += list(range(3072, 3840))
    cols_tok = list(range(1280, 1792)) + list(range(1792, 2304)) + list(range(3840, 4608)) + list(range(1024, 1280))
    cols_in = np.array(cols_feat + cols_tok)
    up_cols = []
    for i in range(43):
        up_cols += list(range(i * 128, i * 128 + 128)) + list(range(DFF + i * 128, DFF + i * 128 + 128))
    up_cols = np.array(up_cols)
    W = np.zeros((depth, NPIECE, 128, 8192), np.float32)
    for l in range(depth):
        wi = inp["w_in"][l][:, cols_in].reshape(16, 128, 9, 512).transpose(2, 1, 0, 3)
        W[l, P_IN:P_IN + 9] = wi.reshape(9, 128, 8192)
        wo = inp["w_out"][l].reshape(16, 128, 4, 512).transpose(2, 1, 0, 3)
        W[l, P_OUT:P_OUT + 4] = wo.reshape(4, 128, 8192)
        wu = np.zeros((D, 22 * 512), np.float32)
        wu[:, :2 * DFF] = inp["w_up"][l][:, up_cols]
        W[l, P_UP:P_UP + 22] = wu.reshape(16, 128, 22, 512).transpose(2, 1, 0, 3).reshape(22, 128, 8192)
        wd = np.zeros((48 * 128, D), np.float32)
        wd[:DFF] = inp["w_down"][l]
        wd = wd.reshape(3, 16, 128, 4, 512).transpose(3, 0, 2, 1, 4)
        W[l, P_DN:P_DN + 12] = wd.reshape(12, 128, 8192)
    gcol = np.zeros((depth, 128, 48), np.float32)
    cw = np.zeros((depth, 128, NFC * 4), np.float32)
    for l in range(depth):
        gm = np.concatenate([inp["gn_a"][l], inp["gn_b"][l], inp["gn_c"][l]])
        for j, g in enumerate([inp["norm1_g"][l], gm, inp["norm2_g"][l]]):
            gcol[l, :, 16 * j:16 * j + 16] = g.reshape(16, 128).T
        c4 = np.concatenate([inp["conv_w"][l], inp["conv_b"][l][None]], 0)[:, up_cols]
        cw[l] = c4.reshape(4, NFC, 128).transpose(2, 1, 0).reshape(128, NFC * 4)
    return W, gcol, cw


def _host_tables(inp, depth):
    s = np.arange(384)[:, None]
    q = np.arange(128)[None, :]
    rel = s - 128 - q
    tab = inp["rel_bias"][_t5_bucket(rel)]
    tab = tab[:, :, QA_ORDER]
    tabA = tab.reshape(3, 128, 128, 12).transpose(1, 3, 0, 2)
    tabA = np.ascontiguousarray(tabA).reshape(128, 12 * 384).astype(np.float32)
    mA = (np.abs(rel) <= 128).astype(np.float32).reshape(3, 128, 128).transpose(1, 0, 2).reshape(128, 384)
    mA = np.ascontiguousarray(np.broadcast_to(mA[:, None, :], (128, 12, 384))).reshape(128, 12 * 384)
    sr = (np.arange(128) // 64)[:, None, None]
    sc = (np.arange(128) % 64)[:, None, None]
    cp = (np.arange(7) - 3)[None, :, None]
    qr = (np.arange(128) // 64)[None, None, :]
    qc = (np.arange(128) % 64)[None, None, :]
    dr = 2 * cp + sr - qr
    ri = np.clip(dr + 7, 0, 14)
    ci = np.clip(sc - qc + 15, 0, 30)
    ri, ci = np.broadcast_arrays(ri, ci)
    tabC = np.zeros((depth, 128, 12 * 7 * 128), np.float32)
    for l in range(depth):
        t = inp["na_rpb"][l][:, ri, ci]
        tabC[l] = t.transpose(1, 0, 2, 3).reshape(128, -1)
    cstart = np.clip(np.arange(64) - 8, 0, 48)
    scv = (np.arange(128) % 64)[:, None]
    qcv = np.arange(128) % 64
    cm = ((scv >= cstart[qcv][None, :]) & (scv < cstart[qcv][None, :] + 16)).astype(np.float32)
    mC = np.ascontiguousarray(np.broadcast_to(cm[:, None, :], (128, 84, 128))).reshape(128, 84 * 128)
    srv = (np.arange(128) // 64)[:, None]
    qrv = (np.arange(128) // 64)[None, :]
    bad = (srv == 0) & (qrv == 1)
    e_lo = cm * (~bad).astype(np.float32)
    e_hi = cm * bad.astype(np.float32)
    mE = np.stack([e_lo, e_hi], 1)
    mE = np.ascontiguousarray(np.broadcast_to(mE[:, None], (128, 12, 2, 128))).reshape(128, 24 * 128).astype(np.float32)
    return tabA, mA, tabC, mC, mE


def _na_rowmask(seq_rows, row0, ntiles):
    m = np.zeros((128, ntiles, 7, 2), np.float32)
    for j in range(ntiles):
        for qr in range(2):
            r = row0 + 2 * j + qr
            if 0 <= r < seq_rows:
                st = min(max(r - 4, 0), seq_rows - 8)
            else:
                st = r - 4
            for c in range(7):
                for sr in range(2):
                    kr = row0 + 2 * (j + c - 3) + sr
                    if st <= kr < st + 8:
                        m[sr * 64:(sr + 1) * 64, j, c, qr] = 1.0
    return m.reshape(128, ntiles * 14)


_STOP = [None]
_DBG = set()


def build(depth, Ts):
    nc = bass.Bass("TRN2", target_bir_lowering=False)
    H = HALO_L * depth
    nseg = len(Ts)
    Es = [T + 2 * H for T in Ts]
    NTs = [E // 128 for E in Es]
    NTtot = sum(NTs)

    def din(name, shape, dt=F32):
        return nc.dram_tensor(name, list(shape), dt, kind="ExternalInput").ap()

    def dscr(name, shape, dt):
        kind = "ExternalOutput" if name in _DBG else "Internal"
        return nc.dram_tensor(name, list(shape), dt, kind=kind).ap()

    xin = [din("x%d" % s, [Es[s], D]) for s in range(nseg)]
    yout = [nc.dram_tensor("y%d" % s, [Ts[s], D], F32, kind="ExternalOutput").ap() for s in range(nseg)]
    wf32 = din("wf32", [depth, NPIECE, 128, 8192])
    gcol_d = din("gcol", [depth, 128, 48])
    cw_d = din("cw", [depth, 128, NFC * 4])
    tabA_d = din("tabA", [128, 12 * 384])
    mA_d = din("mA", [128, 12 * 384])
    tabC_d = din("tabC", [depth, 128, 84 * 128])
    mC_d = din("mC", [128, 84 * 128])
    mE_d = din("mE", [128, 24 * 128])
    rowm_d = din("rowm", [128, NTtot * 14])
    valid_d = din("valid", [128, NTtot])
    flags_d = din("flags", [128, 2])
    ident_d = din("ident", [128, 128], BF16)
    wsT_d = din("wsT", [depth, 128, 8 * 128])
    bs_d = din("bs", [depth, 128, 8])
    lng_d = din("lng", [depth, 128, 512])
    lnb_d = din("lnb", [depth, 128, 512])
    sink_d = din("sink", [depth, 128, 12])
    fg_d = din("fg", [128, D])

    wbf = [dscr("wbf%d" % l_, [NPIECE, 128, 8192], BF16) for l_ in range(depth)]
    hbuf = [dscr("hbuf%d" % s, [Es[s], D], F32) for s in range(nseg)]
    hmid = [dscr("hmid%d" % s, [Es[s], D], F32) for s in range(nseg)]
    featT = [dscr("featT%d" % s, [20, 128, Es[s]], BF16) for s in range(nseg)]
    UB = [dscr("ub%d" % s, [Es[s], 512], F32) for s in range(nseg)]
    VB = [dscr("vb%d" % s, [Es[s], 512], BF16) for s in range(nseg)]
    VCd = [dscr("vc%d" % s, [Es[s], 780], BF16) for s in range(nseg)]
    VAd = [dscr("va%d" % s, [Es[s], 260], BF16) for s in range(nseg)]

    dbgst = dscr("dbgst", [64, 128, 8], F32) if "dbgst" in _DBG else None
    dbgn = [0]

    sc = S(nc)
    pe, act, dve, pool, sp = sc.pe, sc.act, sc.dve, sc.pool, sc.sp

    def dump_stat(st):
        if dbgst is not None and dbgn[0] < 64:
            sc.dma(sp, dbgst[dbgn[0]], st[:, :], st, reads=[st], writes=[sc.dbuf(("dbg", dbgn[0]))])
            dbgn[0] += 1

    psum = nc.alloc_psum_tensor("psum", [128, 4096], F32)
    PS = [Buf("ps%d" % b) for b in range(8)]

    def psf(b, n=512, off=0):
        return psum[:, b * 512 + off: b * 512 + off + n]

    def psb(b0, ncol, off=0):
        v = psum[:, b0 * 512:(b0 + (ncol + 1023) // 1024) * 512].bitcast(BF16)
        return v[:, off:off + ncol]

    uniq = [0]

    def sb(es, name, shape, dt):
        uniq[0] += 1
        t = es.enter_context(nc.sbuf_tensor("sb%d_%s" % (uniq[0], name), list(shape), dt))
        return Buf(name, t)

    g_es = ExitStack()
    ident = sb(g_es, "ident", [128, 128], BF16)
    validt = sb(g_es, "validt", [128, NTtot], F32)
    flags = sb(g_es, "flags", [128, 2], F32)
    rowm = sb(g_es, "rowm", [128, NTtot * 14], F32)
    stat = [sb(g_es, "stat%d" % i, [128, 8], F32) for i in range(4)]
    epsb = sb(g_es, "epsb", [128, 1], F32)
    for b_, d_ in ((ident, ident_d), (validt, valid_d), (flags, flags_d), (rowm, rowm_d)):
        sc.dma(sp, b_[:], d_[:, :], b_, writes=[b_])
    sc.op(dve, lambda e: e.memset(epsb[:], EPS), writes=[epsb])

    statn = [0]

    def rstd_from_ss(ss_ap, ss_buf, n):
        st = stat[statn[0] % 4]
        statn[0] += 1
        sc.op(act, lambda e: e.activation(out=st[:, 1:2], in_=ss_ap, func=AF.Sqrt, bias=epsb[:, 0:1], scale=1.0 / n),
              reads=[ss_buf, epsb], writes=[st])
        sc.op(dve, lambda e: e.reciprocal(out=st[:, 2:3], in_=st[:, 1:2]), reads=[st], writes=[st], tiny=True)
        return st, st[:, 2:3]

    with ExitStack() as es:
        cin = [sb(es, "cin%d" % i, [128, 8192], F32) for i in range(2)]
        cout = [sb(es, "cout%d" % i, [128, 8192], BF16) for i in range(2)]
        gc = sb(es, "gc", [128, 48 * depth], F32)
        for l in range(depth):
            sc.dma(sp, gc[:, 48 * l:48 * l + 48], gcol_d[l], gc, writes=[gc])
        plist = [(0, p) for p in range(min(NPIECE, _STOP[1] if len(_STOP) > 1 else NPIECE))]

        def cast_load(n):
            l_, p_ = plist[n]
            ci_ = cin[n % 2]
            sc.dma((sp, act)[n % 2], ci_[:], wf32[l_, p_, :, :], ci_, writes=[ci_])

        cast_load(0)
        for n, (l, p) in enumerate(plist):
            if n + 1 < len(plist):
                cast_load(n + 1)
            ci_, co_ = cin[n % 2], cout[n % 2]
            goff = None if p >= P_DN else (0 if p < P_OUT else (16 if p < P_UP else 32))
            for (eng, k0, k1) in ((dve, 0, 10), (pool, 10, 16)):
                if goff is None:
                    sc.op(eng, lambda e, ci_=ci_, co_=co_, k0=k0, k1=k1: e.tensor_copy(out=co_[:, k0 * 512:k1 * 512], in_=ci_[:, k0 * 512:k1 * 512]),
                          reads=[ci_], writes=[co_])
                else:
                    fns = []
                    for k in range(k0, k1):
                        gap = gc[:, 48 * l + goff + k: 48 * l + goff + k + 1]
                        fns.append(lambda e, k=k, gap=gap, ci_=ci_, co_=co_: e.tensor_scalar(
                            out=co_[:, k * 512:(k + 1) * 512], in0=ci_[:, k * 512:(k + 1) * 512],
                            scalar1=gap, scalar2=None, op0=ALU.mult))
                    sc.op(eng, fns, reads=[ci_, gc], writes=[co_])
            sc.dma(sp, wbf[l][p, :, :], co_[:], co_, reads=[co_], writes=[sc.dbuf(("w", l, p))])
        sc.barrier()
        sc.release(cin + cout + [gc])

    wtoks = {}
    if _STOP[0] == 'cast':
        sc.barrier()
        return nc

    def wpiece_load(wb, l, p, ncols=8192):
        sc.dma(sp, wb[:, 0:ncols], wbf[l][p, :, 0:ncols], wb, reads=[sc.dbuf(("w", l, p))], writes=[wb])

    def norm_tile(src_ap, src_deps, hb, xs, valid_ap=None):
        sc.dma(sp, hb[:], src_ap, hb, reads=src_deps, writes=[hb])
        st = stat[statn[0] % 4]
        sc.op(act, lambda e: e.activation(out=xs[:], in_=hb[:], func=AF.Square, accum_out=st[:, 0:1]),
              reads=[hb], writes=[xs, st])
        st2, r = rstd_from_ss(st[:, 0:1], st, D)
        sc.op(dve, lambda e: e.tensor_scalar(out=xs[:], in0=hb[:], scalar1=r, scalar2=None, op0=ALU.mult),
              reads=[hb, st2], writes=[xs])
        dump_stat(st2)

    def transpose_tile(xs, xT, col0, banks, ev=0):
        b0 = banks[0]
        pv = psb(b0, 2048)
        sc.op(pe, [lambda e, k=k: e.transpose(out=pv[:, k * 128:(k + 1) * 128], in_=xs[:, k * 128:(k + 1) * 128],
                                              identity=ident[:]) for k in range(16)],
              reads=[xs, ident], writes=[PS[b0], PS[b0 + 1]])
        xv = xT[:].rearrange("p (k t) -> p k t", k=16)
        for hlf in range(2):
            eng = (act, dve)[(hlf + ev) % 2]
            src = pv[:, hlf * 1024:(hlf + 1) * 1024].rearrange("p (k t) -> p k t", k=8)
            dst = xv[:, hlf * 8:(hlf + 1) * 8, col0:col0 + 128]
            if eng is act:
                sc.op(act, lambda e, src=src, dst=dst: e.activation(out=dst, in_=src, func=AF.Copy),
                      reads=[PS[b0 + hlf]], writes=[xT])
            else:
                sc.op(dve, lambda e, src=src, dst=dst: e.tensor_copy(out=dst, in_=src),
                      reads=[PS[b0 + hlf]], writes=[xT])

    def htok(seg, tile_):
        return sc.dbuf(("h", seg, tile_))

    def rows_deps(kind, seg, r0, r1):
        return [sc.dbuf((kind, seg, t)) for t in range(r0 // 128, (r1 - 1) // 128 + 1)]

    tile_base = [0]
    for s_ in range(nseg - 1):
        tile_base.append(tile_base[-1] + NTs[s_])

    class Caster:
        def __init__(self):
            self.jobs, self.nxt, self.loaded, self.bufs = [], 0, None, None

        def start_layer(self, ln):
            self.jobs = [(ln, p, qt) for p in range(NPIECE) for qt in range(4)]
            self.nxt, self.loaded = 0, None

        def _load(self, n):
            ln, p, qt = self.jobs[n]
            ci_ = self.bufs[0][n % 2]
            sc.dma(sp, ci_[:], wf32[ln, p, :, qt * 2048:(qt + 1) * 2048], ci_, writes=[ci_])

        def _cast(self, n):
            ln, p, qt = self.jobs[n]
            ci_, co_, gc_ = self.bufs[0][n % 2], self.bufs[1][n % 2], self.bufs[2]
            goff = None if p >= P_DN else (0 if p < P_OUT else (16 if p < P_UP else 32))
            for (eng, i0, i1) in ((dve, 0, 2), (pool, 2, 4)):
                if goff is None:
                    sc.op(eng, lambda e, i0=i0, i1=i1: e.tensor_copy(out=co_[:, i0 * 512:i1 * 512], in_=ci_[:, i0 * 512:i1 * 512]),
                          reads=[ci_], writes=[co_])
                else:
                    fns = []
                    for i in range(i0, i1):
                        k = qt * 4 + i
                        gap = gc_[:, 48 * ln + goff + k: 48 * ln + goff + k + 1]
                        fns.append(lambda e, i=i, gap=gap: e.tensor_scalar(out=co_[:, i * 512:(i + 1) * 512], in0=ci_[:, i * 512:(i + 1) * 512],
                                                                          scalar1=gap, scalar2=None, op0=ALU.mult))
                    sc.op(eng, fns, reads=[ci_, gc_], writes=[co_])
            sc.dma(pool, wbf[ln][p, :, qt * 2048:(qt + 1) * 2048], co_[:], co_, reads=[co_], writes=[sc.dbuf(("w", ln, p))])

        def point(self):
            if self.bufs is None:
                return
            tc = self.loaded
            if self.nxt < len(self.jobs):
                self._load(self.nxt)
                self.loaded = self.nxt
                self.nxt += 1
            else:
                self.loaded = None
            if tc is not None:
                self._cast(tc)

        def drain(self):
            if self.bufs is not None and self.loaded is not None:
                self._cast(self.loaded)
                self.loaded = None

        def flush(self):
            while self.bufs is not None and (self.nxt < len(self.jobs) or self.loaded is not None):
                self.point()

    caster = Caster()

    for l in range(depth):
        rem = depth - 1 - l
        a = HALO_L * rem
        last = (l == depth - 1)
        for seg in range(nseg):
            T = Ts[seg]
            E = Es[seg]
            f0, f1 = H - a, H + T + a
            m0, m1 = f0 - 128, f1 + 128
            p0, p1 = m0 - 256, m1 + 256
            hsrc = xin[seg] if l == 0 else hbuf[seg]
            hkind = "x" if l == 0 else "h"

            with ExitStack() as es:
                hb = [sb(es, "hb%d" % i, [128, D], F32) for i in range(2)]
                xs = [sb(es, "xs%d" % i, [128, D], BF16) for i in range(2)]
                xT = [sb(es, "xT%d" % i, [128, 16 * 512], BF16) for i in range(2)]
                wb = [sb(es, "wb%d" % i, [128, 8192], BF16) for i in range(2)]
                stf = [sb(es, "stf%d" % i, [128, 512], BF16) for i in range(3)]
                stt = [sb(es, "stt%d" % i, [128, 512], F32) for i in range(2)]
                stb = [sb(es, "stb%d" % i, [128, 512], BF16) for i in range(2)]
                g1 = sb(es, "g1", [128, 512], F32)
                g2 = sb(es, "g2", [128, 512], F32)
                vcs = [sb(es, "vcs%d" % i, [128, 780], BF16) for i in range(4)]
                vas = [sb(es, "vas%d" % i, [128, 260], BF16) for i in range(4)]
                lng = sb(es, "lng", [128, 512], F32)
                lnb = sb(es, "lnb", [128, 512], F32)
                allb = hb + xs + xT + wb + stf + stt + stb + [g1, g2, lng, lnb] + vcs + vas
                sc.dma(sp, lng[:], lng_d[l], lng, writes=[lng])
                sc.dma(sp, lnb[:], lnb_d[l], lnb, writes=[lnb])
                ntile = (p1 - p0) // 128
                gi = 0
                nw = 0
                nst = 0
                bank = 0
                for gt in range(0, ntile, 4):
                    ng = min(4, ntile - gt)
                    N = 128 * ng
                    t0 = p0 + gt * 128
                    xt_ = xT[gi % 2]
                    xv = xt_[:].rearrange("p (k t) -> p k t", k=16)
                    for i in range(ng):
                        r0 = t0 + i * 128
                        norm_tile(hsrc[r0:r0 + 128, :], rows_deps(hkind, seg, r0, r0 + 128), hb[i % 2], xs[i % 2])
                        transpose_tile(xs[i % 2], xt_, i * 128, (0, 1) if i % 2 == 0 else (2, 3), ev=i)
                    for p in range(5):
                        w_ = wb[nw % 2]
                        nw += 1
                        wpiece_load(w_, l, P_IN + p)
                        wv = w_[:].rearrange("p (k c) -> p k c", k=16)
                        for cc in range(4):
                            c = 4 * p + cc
                            bk = 4 + bank % 4
                            bank += 1
                            sc.op(pe, [lambda e, k=k, cc=cc, bk=bk, wv=wv: e.matmul(
                                psf(bk, N), wv[:, k, cc * 128:(cc + 1) * 128], xv[:, k, 0:N],
                                start=(k == 0), stop=(k == 15)) for k in range(16)],
                                reads=[w_, xt_], writes=[PS[bk]])
                            so = stf[nst % 3]
                            nst += 1
                            if c % 2 == 0:
                                sc.op(act, lambda e, so=so, bk=bk: e.activation(out=so[:, 0:N], in_=psf(bk, N), func=AF.Copy),
                                      reads=[PS[bk]], writes=[so])
                            else:
                                sc.op(dve, lambda e, so=so, bk=bk: e.tensor_copy(out=so[:, 0:N], in_=psf(bk, N)),
                                      reads=[PS[bk]], writes=[so])
                            sc.dma(pool, featT[seg][c, :, t0:t0 + N], so[:, 0:N], so, reads=[so],
                                   writes=[sc.dbuf(("f", seg, (t0 // 128) + i_)) for i_ in range(ng)])
                    for p in range(4):
                        w_ = wb[nw % 2]
                        nw += 1
                        wpiece_load(w_, l, P_IN + 5 + p)
                        wv = w_[:].rearrange("p (k c) -> p k c", k=16)
                        for i in range(ng):
                            r0 = t0 + i * 128
                            tl = r0 // 128
                            vcol = validt[:, tile_base[seg] + tl: tile_base[seg] + tl + 1]
                            bk = 4 + bank % 4
                            bank += 1
                            sc.op(pe, [lambda e, k=k, i=i, bk=bk, wv=wv: e.matmul(
                                psf(bk), xv[:, k, i * 128:(i + 1) * 128], wv[:, k, :],
                                start=(k == 0), stop=(k == 15)) for k in range(16)],
                                reads=[w_, xt_], writes=[PS[bk]])
                            if p == 0:
                                so = stt[i % 2]
                                sc.op(act, lambda e, so=so, bk=bk: e.activation(out=so[:], in_=psf(bk), func=AF.Gelu_apprx_tanh),
                                      reads=[PS[bk]], writes=[so])
                                sc.dma(pool, UB[seg][r0:r0 + 128, :], so[:], so, reads=[so], writes=[sc.dbuf(("ub", seg, tl))])
                            elif p == 1:
                                st = stat[statn[0] % 4]
                                statn[0] += 1
                                sc.op(act, lambda e, bk=bk, st=st: e.activation(out=g1[:], in_=psf(bk), func=AF.Gelu_apprx_tanh,
                                                                             accum_out=st[:, 0:1]),
                                      reads=[PS[bk]], writes=[g1, st])
                                sc.op(act, lambda e, st=st: e.activation(out=g2[:], in_=g1[:], func=AF.Square, accum_out=st[:, 1:2]),
                                      reads=[g1], writes=[g2, st])
                                sc.op(act, lambda e, st=st: e.activation(out=st[:, 7:8], in_=st[:, 1:2], func=AF.Copy),
                                      reads=[st], writes=[st])
                                sc.op(dve, [
                                    lambda e, st=st: e.tensor_scalar(out=st[:, 2:3], in0=st[:, 0:1], scalar1=1.0 / 512, scalar2=None, op0=ALU.mult),
                                    lambda e, st=st: e.tensor_tensor(out=st[:, 3:4], in0=st[:, 0:1], in1=st[:, 0:1], op=ALU.mult),
                                ], reads=[st], writes=[st], tiny=True)
                                sc.op(dve, lambda e, st=st: e.scalar_tensor_tensor(out=st[:, 4:5], in0=st[:, 3:4], scalar=-1.0 / 512, in1=st[:, 1:2],
                                                                                  op0=ALU.mult, op1=ALU.add),
                                      reads=[st], writes=[st], tiny=True)
                                sc.op(act, lambda e, st=st: e.activation(out=st[:, 5:6], in_=st[:, 4:5], func=AF.Sqrt, bias=epsb[:, 0:1], scale=1.0 / 512),
                                      reads=[st, epsb], writes=[st])
                                so = stb[i % 2]
                                sc.op(dve, lambda e, st=st: e.reciprocal(out=st[:, 6:7], in_=st[:, 5:6]), reads=[st], writes=[st], tiny=True)
                                sc.op(dve, [
                                    lambda e, st=st: e.tensor_scalar(out=g2[:], in0=g1[:], scalar1=st[:, 2:3], scalar2=st[:, 6:7],
                                                                     op0=ALU.subtract, op1=ALU.mult),
                                    lambda e: e.tensor_tensor(out=g2[:], in0=g2[:], in1=lng[:], op=ALU.mult),
                                    lambda e, so=so: e.tensor_tensor(out=so[:], in0=g2[:], in1=lnb[:], op=ALU.add),
                                ], reads=[st, g1, g2, lng, lnb], writes=[g2, so])
                                sc.dma(pool, VB[seg][r0:r0 + 128, :], so[:], so, reads=[so], writes=[sc.dbuf(("vb", seg, tl))])
                                dump_stat(st)
                            elif p == 2:
                                vc_ = vcs[i]
                                vv = vc_[:].rearrange("p (h d) -> p h d", h=12)
                                sc.op(dve, lambda e, vv=vv, bk=bk, vcol=vcol: e.tensor_scalar(
                                    out=vv[:, 0:8, 0:64], in0=psf(bk).rearrange("p (h d) -> p h d", h=8),
                                    scalar1=vcol, scalar2=None, op0=ALU.mult),
                                    reads=[PS[bk], validt], writes=[vc_])
                            else:
                                vc_ = vcs[i]
                                va_ = vas[i]
                                vv = vc_[:].rearrange("p (h d) -> p h d", h=12)
                                av = va_[:].rearrange("p (h d) -> p h d", h=4)
                                pv_ = psf(bk).rearrange("p (h d) -> p h d", h=8)
                                vb1 = bass.AP(validt.t[:].tensor, vcol.offset, [list(vcol.ap[0]), [0, 12], [1, 1]])
                                vb2 = bass.AP(validt.t[:].tensor, vcol.offset, [list(vcol.ap[0]), [0, 4], [1, 1]])
                                sc.op(dve, [
                                    lambda e, vv=vv, pv_=pv_, vcol=vcol: e.tensor_scalar(out=vv[:, 8:12, 0:64], in0=pv_[:, 0:4, :],
                                                                                      scalar1=vcol, scalar2=None, op0=ALU.mult),
                                    lambda e, av=av, pv_=pv_, vcol=vcol: e.tensor_scalar(out=av[:, :, 0:64], in0=pv_[:, 4:8, :],
                                                                                      scalar1=vcol, scalar2=None, op0=ALU.mult),
                                    lambda e, vv=vv, vb1=vb1: e.tensor_copy(out=vv[:, :, 64:65], in_=vb1),
                                    lambda e, av=av, vb2=vb2: e.tensor_copy(out=av[:, :, 64:65], in_=vb2),
                                ], reads=[PS[bk], validt], writes=[vc_, va_])
                                sc.dma(pool, VCd[seg][r0:r0 + 128, :], vc_[:], vc_, reads=[vc_], writes=[sc.dbuf(("vc", seg, tl))])
                                sc.dma(pool, VAd[seg][r0:r0 + 128, :], va_[:], va_, reads=[va_], writes=[sc.dbuf(("va", seg, tl))])
                    gi += 1
                sc.barrier()
                sc.release(allb)
            if _STOP[0] == 'P':
                return nc

            with ExitStack() as es:
                TA = sb(es, "TA", [128, 12 * 384], BF16)
                TBw = sb(es, "TBw", [128, 84 * 128], BF16)
                TBe = sb(es, "TBe", [128, 24 * 128], BF16)
                qa = [sb(es, "qa%d" % i, [128, 2 * 6 * 128], BF16) for i in range(2)]
                ka = [sb(es, "ka%d" % i, [128, 2 * 384], BF16) for i in range(2)]
                va = [sb(es, "va%d" % i, [128, 3 * 260], BF16) for i in range(2)]
                qc = sb(es, "qc", [128, 2 * 6 * 128], BF16)
                kc = sb(es, "kc", [128, 6 * 896], BF16)
                vc = sb(es, "vc", [128, 7 * 780], BF16)
                gu = sb(es, "gu", [128, 512], F32)
                vn = sb(es, "vn", [128, 512], BF16)
                PT = [sb(es, "PT%d" % i, [128, 1024], BF16) for i in range(2)]
                ycat = sb(es, "ycat", [128, D], F32)
                mrg = sb(es, "mrg", [128, D], BF16)
                mT = sb(es, "mT", [128, 16 * 512], BF16)
                wo = [sb(es, "wo%d" % i, [128, 8192], BF16) for i in range(2)]
                hq = [sb(es, "hq%d" % i, [128, D], F32) for i in range(4)]
                wsT = sb(es, "wsT", [128, 8 * 128], BF16)
                bsb = sb(es, "bsb", [128, 8], F32)
                esk = sb(es, "esk", [128, 12], F32)
                den = sb(es, "den", [128, 24], F32)
                allb = [TA, TBw, TBe, qc, kc, vc, gu, vn, ycat, mrg, mT, wsT, bsb, esk, den] + PT + wo + hq + qa + ka + va
                tmpf, tmpm = hq[0], hq[1]
                tcv = tabC_d[l].rearrange("p (h c q) -> p h c q", h=12, c=7)
                jobs = [(tabA_d[:, c0:min(c0 + 2048, 4608)], mA_d[:, c0:min(c0 + 2048, 4608)], TA, c0, min(2048, 4608 - c0)) for c0 in range(0, 4608, 2048)]
                jobs += [(tabC_d[l][:, c0:min(c0 + 2048, 10752)], mC_d[:, c0:min(c0 + 2048, 10752)], TBw, c0, min(2048, 10752 - c0)) for c0 in range(0, 10752, 2048)]
                for (tab_ap, m_ap, dst, c0, cn) in jobs:
                    sc.dma(sp, tmpf[:, 0:cn], tab_ap[:, 0:cn], tmpf, writes=[tmpf])
                    sc.dma(sp, tmpm[:, 0:cn], m_ap[:, 0:cn], tmpm, writes=[tmpm])
                    sc.op(dve, [lambda e, cn=cn: e.tensor_scalar(out=tmpm[:, 0:cn], in0=tmpm[:, 0:cn], scalar1=-1.0, scalar2=1e30,
                                                                 op0=ALU.add, op1=ALU.mult),
                                lambda e, cn=cn, c0=c0, dst=dst: e.scalar_tensor_tensor(out=dst[:, c0:c0 + cn], in0=tmpf[:, 0:cn], scalar=8.0,
                                                                                      in1=tmpm[:, 0:cn], op0=ALU.mult, op1=ALU.add)],
                          reads=[tmpf, tmpm], writes=[tmpm, dst])
                for hh in range(0, 12, 6):
                    tv_ = tmpf[:, 0:1536].rearrange("p (h c q) -> p h c q", h=6, c=2)
                    for ci_, cw_ in ((0, 1), (1, 5)):
                        sc.dma(sp, tv_[:, :, ci_, :], tcv[:, hh:hh + 6, cw_, :], tmpf, writes=[tmpf])
                    sc.dma(sp, tmpm[:, 0:1536], mE_d[:, hh * 256:hh * 256 + 1536], tmpm, writes=[tmpm])
                    sc.op(dve, [lambda e: e.tensor_scalar(out=tmpm[:, 0:1536], in0=tmpm[:, 0:1536], scalar1=-1.0, scalar2=1e30,
                                                          op0=ALU.add, op1=ALU.mult),
                                lambda e, hh=hh: e.scalar_tensor_tensor(out=TBe[:, hh * 256:hh * 256 + 1536], in0=tmpf[:, 0:1536], scalar=8.0,
                                                                        in1=tmpm[:, 0:1536], op0=ALU.mult, op1=ALU.add)],
                          reads=[tmpf, tmpm], writes=[tmpm, TBe])
                sc.op(dve, [lambda e: e.memset(qa[0][:], 0.0), lambda e: e.memset(qa[1][:], 0.0), lambda e: e.memset(qc[:], 0.0)],
                      writes=[qa[0], qa[1], qc])
                sc.dma(sp, hq[2][:, 0:1024], wsT_d[l], hq[2], writes=[hq[2]])
                sc.op(dve, lambda e: e.tensor_copy(out=wsT[:], in_=hq[2][:, 0:1024]), reads=[hq[2]], writes=[wsT])
                sc.dma(sp, bsb[:], bs_d[l], bsb, writes=[bsb])
                sc.dma(sp, esk[:], sink_d[l], esk, writes=[esk])
                sc.op(act, lambda e: e.activation(out=esk[:], in_=esk[:], func=AF.Exp), reads=[esk], writes=[esk])

                mtile = (m1 - m0) // 128
                pj0, pj1 = p0 // 128, p1 // 128
                specials = (H // 128, H // 128 + 1, (H + T) // 128 - 2, (H + T) // 128 - 1)
                nwo = 0
                nbat = 0
                for gt in range(0, mtile, 4):
                    ng = min(4, mtile - gt)
                    for i in range(ng):
                        j = m0 // 128 + gt + i
                        t0 = j * 128
                        jp = j % 2
                        qa_, ka_, va_ = qa[jp], ka[jp], va[jp]
                        special = j in specials
                        for (qb_, fc0) in ((qa_, 0), (qc, 8)):
                            qv_ = qb_[:].rearrange("p (z c t) -> p z c t", z=2, c=6)
                            for hf in range(2):
                                sc.dma(sp, qv_[hf * 64:(hf + 1) * 64, hf, :, :],
                                       featT[seg][fc0:fc0 + 6, hf * 64:(hf + 1) * 64, t0:t0 + 128].rearrange("c p t -> p c t"),
                                       qb_, reads=[sc.dbuf(("f", seg, j))], writes=[qb_])
                        ca = [c for c in range(3) if pj0 <= j + c - 1 < pj1]
                        cc_ = [c for c in (range(7) if special else range(1, 6)) if pj0 <= j + c - 3 < pj1]
                        ja0, ja1 = j + ca[0] - 1, j + ca[-1] - 1
                        jc0, jc1 = j + cc_[0] - 3, j + cc_[-1] - 3
                        sc.dma(sp, ka_[:].rearrange("p (c t) -> p c t", c=2)[:, :, ca[0] * 128:(ca[-1] + 1) * 128],
                               featT[seg][6:8, :, ja0 * 128:(ja1 + 1) * 128].rearrange("c p t -> p c t"), ka_,
                               reads=[sc.dbuf(("f", seg, x)) for x in range(ja0, ja1 + 1)], writes=[ka_])
                        sc.dma(sp, va_[:].rearrange("p (c f) -> p c f", c=3)[:, ca[0]:ca[-1] + 1, :],
                               VAd[seg][ja0 * 128:(ja1 + 1) * 128, :].rearrange("(c p) f -> p c f", p=128), va_,
                               reads=[sc.dbuf(("va", seg, x)) for x in range(ja0, ja1 + 1)], writes=[va_])
                        sc.dma(sp, kc[:].rearrange("p (c t) -> p c t", c=6)[:, :, cc_[0] * 128:(cc_[-1] + 1) * 128],
                               featT[seg][14:20, :, jc0 * 128:(jc1 + 1) * 128].rearrange("c p t -> p c t"), kc,
                               reads=[sc.dbuf(("f", seg, x)) for x in range(jc0, jc1 + 1)], writes=[kc])
                        sc.dma(sp, vc[:].rearrange("p (c f) -> p c f", c=7)[:, cc_[0]:cc_[-1] + 1, :],
                               VCd[seg][jc0 * 128:(jc1 + 1) * 128, :].rearrange("(c p) f -> p c f", p=128), vc,
                               reads=[sc.dbuf(("vc", seg, x)) for x in range(jc0, jc1 + 1)], writes=[vc])
                        sc.dma(sp, gu[:], UB[seg][t0:t0 + 128, :], gu, reads=[sc.dbuf(("ub", seg, j))], writes=[gu])
                        sc.dma(sp, vn[:], VB[seg][t0:t0 + 128, :], vn, reads=[sc.dbuf(("vb", seg, j))], writes=[vn])
                        qav = qa_[:].rearrange("p (z c t) -> p z c t", z=2, c=6)
                        kav = ka_[:].rearrange("p (c t) -> p c t", c=2)
                        vav = va_[:].rearrange("p (c f) -> p c f", c=3)
                        qcv = qc[:].rearrange("p (z c t) -> p z c t", z=2, c=6)
                        kcv = kc[:].rearrange("p (c t) -> p c t", c=6)
                        vcv = vc[:].rearrange("p (c f) -> p c f", c=7)

                        def ocol(h):
                            return (4, h * 65) if h < 7 else (5, (h - 7) * 65)

                        def finish_heads(dst0, sink):
                            d7 = den[:, 0:7]
                            d5 = den[:, 7:12]
                            o7 = psf(4, 455).rearrange("p (h d) -> p h d", h=7)
                            o5 = psf(5, 325).rearrange("p (h d) -> p h d", h=5)
                            if sink:
                                fns = [lambda e: e.tensor_tensor(out=d7.unsqueeze(2), in0=o7[:, :, 64:65], in1=esk[:, 0:7].unsqueeze(2), op=ALU.add),
                                       lambda e: e.tensor_tensor(out=d5.unsqueeze(2), in0=o5[:, :, 64:65], in1=esk[:, 7:12].unsqueeze(2), op=ALU.add)]
                            else:
                                fns = [lambda e: e.tensor_scalar(out=d7.unsqueeze(2), in0=o7[:, :, 64:65], scalar1=1e-30, scalar2=None, op0=ALU.max),
                                       lambda e: e.tensor_scalar(out=d5.unsqueeze(2), in0=o5[:, :, 64:65], scalar1=1e-30, scalar2=None, op0=ALU.max)]
                            sc.op(dve, fns, reads=PS[4:6] + [esk], writes=[den], tiny=True)
                            sc.op(dve, lambda e: e.reciprocal(out=den[:, 12:24], in_=den[:, 0:12]), reads=[den], writes=[den], tiny=True)
                            r7 = bass.AP(den.t[:].tensor, den[:, 12:19].offset, [list(den[:, 12:19].ap[0]), [1, 7], [0, 64]])
                            r5 = bass.AP(den.t[:].tensor, den[:, 19:24].offset, [list(den[:, 19:24].ap[0]), [1, 5], [0, 64]])
                            y7 = ycat[:, dst0:dst0 + 448].rearrange("p (h d) -> p h d", h=7)
                            y5 = ycat[:, dst0 + 448:dst0 + 768].rearrange("p (h d) -> p h d", h=5)
                            sc.op(dve, [lambda e: e.tensor_tensor(out=y7, in0=o7[:, :, 0:64], in1=r7, op=ALU.mult),
                                        lambda e: e.tensor_tensor(out=y5, in0=o5[:, :, 0:64], in1=r5, op=ALU.mult)],
                                  reads=PS[4:6] + [den], writes=[ycat])

                        batches = []
                        for b in range(6):
                            qk, pvf, spans = [], [], []
                            for k in range(2):
                                pi = 2 * b + k
                                half, qch = pi % 2, pi // 2
                                h = QA_ORDER[pi]
                                kvh = h // 3
                                kch = kvh // 2
                                bk, oc = ocol(h)
                                for c in ca:
                                    col = k * 512 + c * 128
                                    qk.append((col, kav[:, kch, c * 128:(c + 1) * 128], qav[:, half, qch, :], TA[:, (pi * 3 + c) * 128:(pi * 3 + c + 1) * 128]))
                                    pvf.append((bk, oc, col, vav[:, c, kvh * 65:(kvh + 1) * 65], c == ca[0], c == ca[-1]))
                                spans.append((k * 512 + ca[0] * 128, k * 512 + (ca[-1] + 1) * 128))
                            batches.append(dict(qk=qk, pv=pvf, spans=spans, rq=[qa_, ka_, TA, ident], rv=[va_], mask=False))
                        for h in range(12):
                            b, k = h // 2, h % 2
                            bk, oc = ocol(h)
                            qk, pvf = [], []
                            for c in cc_:
                                col = c * 128
                                if special or c not in (1, 5):
                                    tb_ = TBw[:, (h * 7 + c) * 128:(h * 7 + c + 1) * 128]
                                else:
                                    e_ = h * 2 + (0 if c == 1 else 1)
                                    tb_ = TBe[:, e_ * 128:(e_ + 1) * 128]
                                qk.append((col, kcv[:, b, c * 128:(c + 1) * 128], qcv[:, k, b, :], tb_))
                                pvf.append((bk, oc, col, vcv[:, c, h * 65:(h + 1) * 65], c == cc_[0], c == cc_[-1]))
                            batches.append(dict(qk=qk, pv=pvf, spans=[(cc_[0] * 128, (cc_[-1] + 1) * 128)], rq=[qc, kc, TBw, TBe, ident], rv=[vc],
                                                mask=special))

                        def emit_qk(bt, bn):
                            base = (bn % 2) * 1024
                            fns = []
                            for (col, kk, qq, tb_) in bt["qk"]:
                                o_ = psum[:, base + col: base + col + 128]
                                fns.append(lambda e, o_=o_, kk=kk, qq=qq: e.matmul(o_, kk, qq, start=True, stop=False))
                                fns.append(lambda e, o_=o_, tb_=tb_: e.matmul(o_, ident[:], tb_, start=False, stop=True))
                            sc.op(pe, fns, reads=bt["rq"], writes=PS[(bn % 2) * 2:(bn % 2) * 2 + 2])

                        def emit_rest(bt, bn):
                            base = (bn % 2) * 1024
                            pt = PT[bn % 2]
                            sc.op(act, [lambda e, a_=a_, b_=b_, pt=pt: e.activation(out=pt[:, a_:b_], in_=psum[:, base + a_: base + b_], func=AF.Exp,
                                                                                  scale=0.125) for (a_, b_) in bt["spans"]],
                                  reads=PS[(bn % 2) * 2:(bn % 2) * 2 + 2], writes=[pt])
                            if bt["mask"]:
                                rmo = (tile_base[seg] + j) * 14
                                rmv = bass.AP(rowm.t[:].tensor, rowm[:, rmo:rmo + 14].offset,
                                              [list(rowm[:, rmo:rmo + 14].ap[0]), [1, 14], [0, 64]])
                                pv_ = pt[:, 0:896].rearrange("p (c q) -> p c q", c=14)
                                sc.op(pool, lambda e, pv_=pv_, rmv=rmv: e.tensor_tensor(out=pv_, in0=pv_, in1=rmv, op=ALU.mult),
                                      reads=[pt, rowm], writes=[pt])
                            sc.op(pe, [lambda e, bk=bk, oc=oc, col=col, vv=vv, s_=s_, t_=t_, pt=pt: e.matmul(
                                psf(bk, 65, oc), pt[:, col:col + 128], vv, start=s_, stop=t_) for (bk, oc, col, vv, s_, t_) in bt["pv"]],
                                reads=[pt] + bt["rv"], writes=PS[4:6])

                        def emit_B():
                            sc.op(pe, [lambda e, g=g: e.matmul(psf(6, 64, g * 64), wsT[:, g * 128:(g + 1) * 128], vn[:, g * 64:(g + 1) * 64],
                                                               start=True, stop=True) for g in range(8)],
                                  reads=[wsT, vn], writes=[PS[6]])
                            sc.op(dve, [lambda e, g=g: e.scalar_tensor_tensor(out=ycat[:, 768 + g * 64:768 + (g + 1) * 64], in0=psf(6, 64, g * 64),
                                                                              scalar=bsb[:, g:g + 1], in1=gu[:, g * 64:(g + 1) * 64],
                                                                              op0=ALU.add, op1=ALU.mult) for g in range(8)],
                                  reads=[PS[6], bsb, gu], writes=[ycat])

                        def emit_gnorm(c0, cn):
                            st = stat[statn[0] % 4]
                            sc.op(act, lambda e, st=st: e.activation(out=mrg[:, c0:c0 + cn], in_=ycat[:, c0:c0 + cn], func=AF.Square,
                                                                     accum_out=st[:, 0:1]),
                                  reads=[ycat], writes=[mrg, st])
                            st2, r = rstd_from_ss(st[:, 0:1], st, cn)
                            sc.op(dve, lambda e, r=r: e.tensor_scalar(out=mrg[:, c0:c0 + cn], in0=ycat[:, c0:c0 + cn], scalar1=r,
                                                                      scalar2=None, op0=ALU.mult),
                                  reads=[ycat, st2], writes=[mrg])

                        emit_qk(batches[0], nbat)
                        for bi, bt in enumerate(batches):
                            if bi + 1 < len(batches):
                                emit_qk(batches[bi + 1], nbat + bi + 1)
                            emit_rest(bt, nbat + bi)
                            if bi == 5:
                                finish_heads(0, True)
                                emit_B()
                        finish_heads(1280, False)
                        nbat += len(batches)
                        emit_gnorm(0, 768)
                        emit_gnorm(768, 512)
                        emit_gnorm(1280, 768)
                        transpose_tile(mrg, mT, i * 128, (6, 7), ev=i)
                    mv = mT[:].rearrange("p (k t) -> p k t", k=16)
                    for dg in range(4):
                        w_ = wo[nwo % 2]
                        nwo += 1
                        wpiece_load(w_, l, P_OUT + dg)
                        wv = w_[:].rearrange("p (k c) -> p k c", k=16)
                        for i in range(ng):
                            r0 = m0 + (gt + i) * 128
                            bk = 6 + (dg * ng + i) % 2
                            sc.op(pe, [lambda e, k=k, i=i, bk=bk, wv=wv: e.matmul(psf(bk), mv[:, k, i * 128:(i + 1) * 128], wv[:, k, :],
                                                                               start=(k == 0), stop=(k == 15)) for k in range(16)],
                                  reads=[mT, w_], writes=[PS[bk]])
                            hr = hq[i]
                            if dg == 0:
                                sc.dma(sp, hr[:], hsrc[r0:r0 + 128, :], hr, reads=rows_deps(hkind, seg, r0, r0 + 128), writes=[hr])
                            sc.op(dve, lambda e, hr=hr, bk=bk, dg=dg: e.tensor_tensor(out=hr[:, dg * 512:(dg + 1) * 512], in0=psf(bk),
                                                                                   in1=hr[:, dg * 512:(dg + 1) * 512], op=ALU.add),
                                  reads=[PS[bk], hr], writes=[hr])
                            if dg == 3:
                                sc.dma(pool, hmid[seg][r0:r0 + 128, :], hr[:], hr, reads=[hr], writes=[sc.dbuf(("m", seg, r0 // 128))])
                sc.barrier()
                sc.release(allb)
            if _STOP[0] == 'M':
                return nc

            with ExitStack() as es:
                hb = [sb(es, "fhb%d" % i, [128, D], F32) for i in range(2)]
                xs = [sb(es, "fxs%d" % i, [128, D], BF16) for i in range(2)]
                xT2 = sb(es, "xT2", [128, 16 * 512], BF16)
                AT = sb(es, "AT", [128, 43 * 512], BF16)
                wb = [sb(es, "fwb%d" % i, [128, 8192], BF16) for i in range(2)]
                tg = [sb(es, "tg%d" % i, [128, 512], F32) for i in range(2)]
                tv = [sb(es, "tv%d" % i, [128, 512], F32) for i in range(2)]
                hout = [sb(es, "hout%d" % i, [128, D], F32) for i in range(4)]
                cwb = sb(es, "cwb", [128, NFC * 4], F32)
                allb = hb + xs + [xT2, AT, cwb] + wb + tg + tv + hout
                sc.dma(sp, cwb[:], cw_d[l], cwb, writes=[cwb])
                if last:
                    fgb = sb(es, "fgb", [128, D], F32)
                    allb.append(fgb)
                    sc.dma(sp, fgb[:], fg_d[:, :], fgb, writes=[fgb])
                else:
                    ccin = [sb(es, "ccin%d" % i, [128, 2048], F32) for i in range(2)]
                    ccout = [sb(es, "ccout%d" % i, [128, 2048], BF16) for i in range(2)]
                    cgc = sb(es, "cgc", [128, 48 * depth], F32)
                    allb += ccin + ccout + [cgc]
                    for l_ in range(depth):
                        sc.dma(sp, cgc[:, 48 * l_:48 * l_ + 48], gcol_d[l_], cgc, writes=[cgc])
                    if seg == 0:
                        caster.start_layer(l + 1)
                    caster.bufs = (ccin, ccout, cgc)
                xv = xT2[:].rearrange("p (k t) -> p k t", k=16)
                starts = list(range(f0, f1, 510))
                nw = 0
                for t0 in starts:
                    w0 = t0 - 1
                    n_ = min(510, f1 - t0)
                    N = n_ + 2
                    nst_ = (n_ + 127) // 128
                    for i in range((N + 127) // 128):
                        r0 = w0 + 128 * i
                        norm_tile(hmid[seg][r0:r0 + 128, :], rows_deps("m", seg, r0, r0 + 128), hb[i % 2], xs[i % 2])
                        transpose_tile(xs[i % 2], xT2, i * 128, (0, 1), ev=i)
                    for (tokpos, fcol) in ((H - 1, 0), (H + T, 1)):
                        cL = tokpos - w0
                        if 0 <= cL < N:
                            sc.op(dve, lambda e, cL=cL, fcol=fcol: e.tensor_scalar(out=xv[:, :, cL:cL + 1], in0=xv[:, :, cL:cL + 1],
                                                                                 scalar1=flags[:, fcol:fcol + 1], scalar2=None, op0=ALU.mult),
                                  reads=[xT2, flags], writes=[xT2])
                    for st_ in range(nst_):
                        M = min(128, n_ - 128 * st_)
                        r0 = t0 + 128 * st_
                        sc.dma(sp, hout[st_][0:M, :], hmid[seg][r0:r0 + M, :], hout[st_], reads=rows_deps("m", seg, r0, r0 + M),
                               writes=[hout[st_]])
                    for p in range(22):
                        if p % 2 == 1:
                            caster.point()
                        w_ = wb[nw % 2]
                        nw += 1
                        wpiece_load(w_, l, P_UP + p)
                        wv = w_[:].rearrange("p (k c) -> p k c", k=16)
                        for cc in range(4):
                            f = 4 * p + cc
                            if f >= NFC:
                                break
                            bk = 2 + f % 2
                            sc.op(pe, [lambda e, k=k, cc=cc, bk=bk, wv=wv: e.matmul(psf(bk, N), wv[:, k, cc * 128:(cc + 1) * 128], xv[:, k, 0:N],
                                                                                 start=(k == 0), stop=(k == 15)) for k in range(16)],
                                  reads=[w_, xT2], writes=[PS[bk]])
                            pi = f // 2
                            gate = (f % 2 == 0)
                            tb = (tg if gate else tv)[pi % 2]
                            cwv = cwb[:, f * 4:(f + 1) * 4]
                            sc.op(act, lambda e, tb=tb, bk=bk, cwv=cwv: e.activation(out=tb[:, 0:n_], in_=psf(bk, n_, 1), func=AF.Identity,
                                                                                  bias=cwv[:, 3:4], scale=cwv[:, 1:2]),
                                  reads=[PS[bk], cwb], writes=[tb])
                            sc.op(dve, [
                                lambda e, tb=tb, bk=bk, cwv=cwv: e.scalar_tensor_tensor(out=tb[:, 0:n_], in0=psf(bk, n_, 0), scalar=cwv[:, 0:1],
                                                                                     in1=tb[:, 0:n_], op0=ALU.mult, op1=ALU.add),
                                lambda e, tb=tb, bk=bk, cwv=cwv: e.scalar_tensor_tensor(out=tb[:, 0:n_], in0=psf(bk, n_, 2), scalar=cwv[:, 2:3],
                                                                                     in1=tb[:, 0:n_], op0=ALU.mult, op1=ALU.add),
                            ], reads=[PS[bk], cwb, tb], writes=[tb])
                            if gate:
                                sc.op(act, lambda e, tb=tb: e.activation(out=tb[:, 0:n_], in_=tb[:, 0:n_], func=AF.Gelu_apprx_tanh),
                                      reads=[tb], writes=[tb])
                            else:
                                tgb = tg[pi % 2]
                                sc.op(pool, lambda e, tb=tb, tgb=tgb, pi=pi: e.tensor_tensor(out=AT[:, pi * 512 + 1:pi * 512 + 1 + n_], in0=tgb[:, 0:n_],
                                                                                         in1=tb[:, 0:n_], op=ALU.mult),
                                      reads=[tb, tgb], writes=[AT])
                    for dg in range(4):
                        caster.point()
                        for pc in range(3):
                            w_ = wb[nw % 2]
                            nw += 1
                            wpiece_load(w_, l, P_DN + dg * 3 + pc)
                            wv = w_[:].rearrange("p (k c) -> p k c", k=16)
                            fns = []
                            for fi in range(16):
                                f = pc * 16 + fi
                                if f >= 43:
                                    break
                                for st_ in range(nst_):
                                    M = min(128, n_ - 128 * st_)
                                    c0 = f * 512 + 1 + 128 * st_
                                    fns.append(lambda e, fi=fi, f=f, st_=st_, M=M, c0=c0, wv=wv: e.matmul(
                                        psum[0:M, (4 + st_) * 512:(5 + st_) * 512], AT[:, c0:c0 + M], wv[:, fi, :],
                                        start=(f == 0), stop=(f == 42)))
                            sc.op(pe, fns, reads=[AT, w_], writes=PS[4:8])
                        for st_ in range(nst_):
                            M = min(128, n_ - 128 * st_)
                            r0 = t0 + 128 * st_
                            ho = hout[st_]
                            sc.op(dve, lambda e, ho=ho, st_=st_, M=M, dg=dg: e.tensor_tensor(
                                out=ho[0:M, dg * 512:(dg + 1) * 512], in0=psum[0:M, (4 + st_) * 512:(5 + st_) * 512],
                                in1=ho[0:M, dg * 512:(dg + 1) * 512], op=ALU.add),
                                reads=[PS[4 + st_], ho], writes=[ho])
                            if dg == 3:
                                if not last:
                                    sc.dma(pool, hbuf[seg][r0:r0 + M, :], ho[0:M, :], ho, reads=[ho], writes=rows_deps("h", seg, r0, r0 + M))
                                else:
                                    st = stat[statn[0] % 4]
                                    xs_ = xs[st_ % 2]
                                    sc.op(act, lambda e, ho=ho, M=M, st=st, xs_=xs_: e.activation(out=xs_[0:M, :], in_=ho[0:M, :], func=AF.Square,
                                                                                             accum_out=st[0:M, 0:1]),
                                          reads=[ho], writes=[xs_, st])
                                    st2, r = rstd_from_ss(st[:, 0:1], st, D)
                                    sc.op(dve, [
                                        lambda e, ho=ho, M=M, st=st: e.tensor_scalar(out=ho[0:M, :], in0=ho[0:M, :], scalar1=st[0:M, 2:3], scalar2=None,
                                                                                   op0=ALU.mult),
                                        lambda e, ho=ho, M=M: e.tensor_tensor(out=ho[0:M, :], in0=ho[0:M, :], in1=fgb[0:M, :], op=ALU.mult),
                                    ], reads=[ho, st2, fgb], writes=[ho])
                                    sc.dma(pool, yout[seg][r0 - H:r0 - H + M, :], ho[0:M, :], ho, reads=[ho], writes=[sc.dbuf(("y", seg, r0))])
                if seg == nseg - 1:
                    caster.flush()
                caster.drain()
                caster.bufs = None
                sc.barrier()
                sc.release(allb)
    sc.barrier()
    return nc


_CACHE = {}


def _run(inp, depth, seqs):
    inp = {k: np.asarray(v) for k, v in inp.items()}
    xs_full = [inp["x_prompt"], inp["x_sample"]]
    Ts = [s // 4 for s in seqs]
    H = HALO_L * depth
    key = (depth, tuple(Ts))
    if key not in _CACHE:
        _CACHE[key] = build(depth, Ts)
    nc = _CACHE[key]
    W, gcol, cw = _host_weights(inp, depth)
    tabA, mA, tabC, mC, mE = _host_tables(inp, depth)
    ident = np.eye(128, dtype=np.float32).astype(ml_dtypes.bfloat16)
    wsT = np.ascontiguousarray(inp["w_spatial"][:depth].transpose(0, 3, 1, 2)).reshape(depth, 128, 1024).astype(np.float32)
    bs = np.ascontiguousarray(inp["b_spatial"][:depth].transpose(0, 2, 1)).astype(np.float32)
    lng = np.ascontiguousarray(np.broadcast_to(inp["sgu_ln_g"][:depth, None, :], (depth, 128, 512))).astype(np.float32)
    lnb = np.ascontiguousarray(np.broadcast_to(inp["sgu_ln_b"][:depth, None, :], (depth, 128, 512))).astype(np.float32)
    sink = np.ascontiguousarray(np.broadcast_to(inp["sink"][:depth, None, :], (depth, 128, 12))).astype(np.float32)
    fg = np.ascontiguousarray(np.broadcast_to(inp["final_g"][None, :], (128, D))).astype(np.float32)
    shared = dict(wf32=W, gcol=gcol, cw=cw, tabA=tabA, mA=mA, tabC=tabC, mC=mC, mE=mE, ident=ident, wsT=wsT, bs=bs, lng=lng,
                  lnb=lnb, sink=sink, fg=fg)
    in_maps = []
    for c in range(8):
        b, q = c // 4, c % 4
        m = dict(shared)
        rowms, valids = [], []
        for s_, (xf, T, seq) in enumerate(zip(xs_full, Ts, seqs)):
            E = T + 2 * H
            lo = q * T - H
            xe = np.zeros((E, D), np.float32)
            a0, a1 = max(lo, 0), min(lo + E, seq)
            xe[a0 - lo:a1 - lo] = xf[b, a0:a1]
            m["x%d" % s_] = xe
            pos = lo + np.arange(E)
            v = ((pos >= 0) & (pos < seq)).astype(np.float32)
            valids.append(v.reshape(E // 128, 128).T)
            rowms.append(_na_rowmask(seq // 64, lo // 64, E // 128))
        m["valid"] = np.ascontiguousarray(np.concatenate(valids, 1))
        m["rowm"] = np.ascontiguousarray(np.concatenate(rowms, 1))
        fl = np.zeros((128, 2), np.float32)
        fl[:, 0] = 1.0 if q > 0 else 0.0
        fl[:, 1] = 1.0 if q < 3 else 0.0
        m["flags"] = fl
        in_maps.append(m)
    res = run_bass_kernel_spmd(nc, in_maps, core_ids=list(range(8)))
    if _DBG:
        return res.results
    outs = [np.zeros((2, seqs[0], D), np.float32), np.zeros((2, seqs[1], D), np.float32)]
    for c in range(8):
        b, q = c // 4, c % 4
        for s_ in range(2):
            outs[s_][b, q * Ts[s_]:(q + 1) * Ts[s_]] = res.results[c]["y%d" % s_]
    return tuple(outs)


def kernel(**inputs):
    return _run(inputs, 4, (16384, 8192))
```
